# Optimizing a Trainium2 kernel written in Bass

```python
import jax, jax.numpy as jnp
from jax import lax
import numpy as np

D_MODEL = 1024
BATCH = 8
SEQ = 2048
DEPTH = 4

MIX_WIDTH = D_MODEL
SGU_CHUNK = 128
SGU_WIDTH = MIX_WIDTH // 2
SGU_GROUPS = 4
SGU_GROUP_DIM = SGU_WIDTH // SGU_GROUPS
ATT_WIDTH = MIX_WIDTH - SGU_WIDTH
ATT_HEAD_DIM = 64
ATT_HEADS = ATT_WIDTH // ATT_HEAD_DIM
IDX_HEADS = 4
IDX_HEAD_DIM = 64
TOPK_MAX = 256
ROPE_THETA = 500000.0
ROPE_FRACTION = 4
D_FF = 2816
Q_BLOCK = 128
RMS_EPS = 1e-6
N_MOD = 9

OFF_SGU_U = 0
OFF_SGU_V = OFF_SGU_U + SGU_WIDTH
OFF_Q = OFF_SGU_V + SGU_WIDTH
OFF_K = OFF_Q + ATT_WIDTH
OFF_V = OFF_K + ATT_WIDTH
OFF_IQ = OFF_V + ATT_WIDTH
OFF_IK = OFF_IQ + IDX_HEADS * IDX_HEAD_DIM
OFF_IW = OFF_IK + IDX_HEAD_DIM
PROJ_WIDTH = OFF_IW + IDX_HEADS

kernel_name = "hymba_gmlp_dsa_macaron_adaln"


def rms_norm(x, g):
    xf = x.astype(jnp.float32)
    y = xf * lax.rsqrt(jnp.mean(xf * xf, axis=-1, keepdims=True) + RMS_EPS)
    return (y * g.astype(jnp.float32)).astype(x.dtype)


def modulate(h, shift, scale):
    return h * (1 + scale[:, None, :]) + shift[:, None, :]


def swiglu(h, w_in, w_out):
    g, u = jnp.split(h @ w_in, 2, axis=-1)
    return (jax.nn.silu(g) * u) @ w_out


def rope_tables(positions, rot_dim):
    inv_freq = ROPE_THETA ** (-jnp.arange(0, rot_dim, 2, dtype=jnp.float32) / rot_dim)
    ang = positions.astype(jnp.float32)[..., None] * inv_freq
    return jnp.cos(ang)[:, :, None, :], jnp.sin(ang)[:, :, None, :]


def partial_rope(x, cos, sin):
    half = cos.shape[-1]
    rot = 2 * half
    xr = x[..., :rot].astype(jnp.float32)
    x1, x2 = xr[..., :half], xr[..., half:]
    out = jnp.concatenate([x1 * cos - x2 * sin, x2 * cos + x1 * sin], axis=-1).astype(x.dtype)
    return jnp.concatenate([out, x[..., rot:]], axis=-1)


def chunked_sgu(u, v, w_s, b_s):
    B, S, _ = u.shape
    n_chunk = S // SGU_CHUNK
    v = v.reshape(B, n_chunk, SGU_CHUNK, SGU_GROUPS, SGU_GROUP_DIM)
    causal = jnp.tril(jnp.ones((SGU_CHUNK, SGU_CHUNK), dtype=bool))
    w = jnp.where(causal[None], w_s, 0).astype(v.dtype)
    mixed = jnp.einsum('gts,bnsgc->bntgc', w, v) + b_s.T[None, None, :, :, None].astype(v.dtype)
    return u * mixed.reshape(B, S, SGU_WIDTH)


def dsa_attention(q, k, v, qi, ki, wi):
    B, S = q.shape[0], q.shape[1]
    top_k = min(TOPK_MAX, S // 4)
    n_blk = S // Q_BLOCK

    def to_blocks(a):
        return a.reshape((B, n_blk, Q_BLOCK) + a.shape[2:]).swapaxes(0, 1)

    t_blocks = jnp.arange(S, dtype=jnp.int32).reshape(n_blk, Q_BLOCK)
    key_pos = jnp.arange(S, dtype=jnp.int32)
    ki_f = ki.astype(jnp.float32)
    scale = ATT_HEAD_DIM ** -0.5

    def one_block(args):
        qb, qib, wib, tb = args
        logits = jax.nn.relu(jnp.einsum('bqhd,bsd->bqhs', qib.astype(jnp.float32), ki_f))
        score = jnp.einsum('bqh,bqhs->bqs', wib.astype(jnp.float32), logits)
        causal = key_pos[None, :] <= tb[:, None]
        score = jnp.where(causal[None], score, -jnp.inf)
        _, idx = lax.top_k(score, top_k)
        valid = idx <= tb[None, :, None]
        kg = jax.vmap(lambda kb, ib: kb[ib])(k, idx)
        vg = jax.vmap(lambda vb, ib: vb[ib])(v, idx)
        s = jnp.einsum('bqhd,bqkhd->bhqk', qb, kg).astype(jnp.float32) * scale
        s = jnp.where(valid[:, None], s, -jnp.inf)
        p = jax.nn.softmax(s, axis=-1).astype(vg.dtype)
        return jnp.einsum('bhqk,bqkhd->bqhd', p, vg)

    out = lax.map(one_block, (to_blocks(q), to_blocks(qi), to_blocks(wi), t_blocks))
    return out.swapaxes(0, 1).reshape(B, S, ATT_WIDTH)


def token_mix(h, cos, sin, w_in, sgu_w, sgu_b, w_out):
    B, S, _ = h.shape
    p = h @ w_in
    u = jax.nn.gelu(p[..., OFF_SGU_U:OFF_SGU_V], approximate=False)
    va = jax.nn.gelu(p[..., OFF_SGU_V:OFF_Q], approximate=False)
    a_out = chunked_sgu(u, va, sgu_w, sgu_b)
    q = partial_rope(p[..., OFF_Q:OFF_K].reshape(B, S, ATT_HEADS, ATT_HEAD_DIM), cos, sin)
    k = partial_rope(p[..., OFF_K:OFF_V].reshape(B, S, ATT_HEADS, ATT_HEAD_DIM), cos, sin)
    v = p[..., OFF_V:OFF_IQ].reshape(B, S, ATT_HEADS, ATT_HEAD_DIM)
    qi = partial_rope(p[..., OFF_IQ:OFF_IK].reshape(B, S, IDX_HEADS, IDX_HEAD_DIM), cos, sin)
    ki = partial_rope(p[..., OFF_IK:OFF_IW][:, :, None, :], cos, sin)[:, :, 0, :]
    wi = p[..., OFF_IW:PROJ_WIDTH]
    b_out = dsa_attention(q, k, v, qi, ki, wi)
    return jnp.concatenate([a_out, b_out], axis=-1) @ w_out


def setup_inputs(seed: int = 0) -> dict:
    key = jax.random.key(seed)
    ks = jax.random.split(key, 24)
    f32 = jnp.float32
    L, D = DEPTH, D_MODEL

    def nrm(k, shape, scale):
        return jax.random.normal(k, shape, f32) * scale

    x = jax.random.normal(ks[0], (BATCH, SEQ, D), f32)
    c = jax.random.normal(ks[1], (BATCH, D), f32)
    positions = (jnp.arange(SEQ, dtype=jnp.int32)[None, :]
                 + jax.random.randint(ks[2], (BATCH, 1), 0, 4096, dtype=jnp.int32))
    return {
        "x": x,
        "c": c,
        "positions": positions,
        "ada_w": nrm(ks[3], (L, D, N_MOD * D), 0.5 * D ** -0.5),
        "ada_b": nrm(ks[4], (L, N_MOD * D), 0.01),
        "norm_ffn1": 1.0 + nrm(ks[5], (L, D), 0.02),
        "ffn1_w_in": nrm(ks[6], (L, D, 2 * D_FF), D ** -0.5),
        "ffn1_w_out": nrm(ks[7], (L, D_FF, D), D_FF ** -0.5),
        "norm_mix": 1.0 + nrm(ks[8], (L, D), 0.02),
        "mix_w_in": nrm(ks[9], (L, D, PROJ_WIDTH), D ** -0.5),
        "sgu_w": nrm(ks[10], (L, SGU_GROUPS, SGU_CHUNK, SGU_CHUNK), SGU_CHUNK ** -0.5),
        "sgu_b": 1.0 + nrm(ks[11], (L, SGU_GROUPS, SGU_CHUNK), 0.02),
        "mix_w_out": nrm(ks[12], (L, MIX_WIDTH, D), MIX_WIDTH ** -0.5),
        "norm_ffn2": 1.0 + nrm(ks[13], (L, D), 0.02),
        "ffn2_w_in": nrm(ks[14], (L, D, 2 * D_FF), D ** -0.5),
        "ffn2_w_out": nrm(ks[15], (L, D_FF, D), D_FF ** -0.5),
        "final_norm": 1.0 + nrm(ks[16], (D,), 0.02),
    }


def reference(x, c, positions, ada_w, ada_b, norm_ffn1, ffn1_w_in, ffn1_w_out,
              norm_mix, mix_w_in, sgu_w, sgu_b, mix_w_out,
              norm_ffn2, ffn2_w_in, ffn2_w_out, final_norm):
    cos, sin = rope_tables(positions, ATT_HEAD_DIM // ROPE_FRACTION)
    cos, sin = cos.astype(x.dtype), sin.astype(x.dtype)
    c_act = jax.nn.silu(c)
    for l in range(DEPTH):
        mod = c_act @ ada_w[l] + ada_b[l]
        sh1, sc1, g1, sh2, sc2, g2, sh3, sc3, g3 = jnp.split(mod, N_MOD, axis=-1)
        h = modulate(rms_norm(x, norm_ffn1[l]), sh1, sc1)
        x = x + 0.5 * g1[:, None, :] * swiglu(h, ffn1_w_in[l], ffn1_w_out[l])
        h = modulate(rms_norm(x, norm_mix[l]), sh2, sc2)
        x = x + g2[:, None, :] * token_mix(h, cos, sin, mix_w_in[l], sgu_w[l], sgu_b[l], mix_w_out[l])
        h = modulate(rms_norm(x, norm_ffn2[l]), sh3, sc3)
        x = x + 0.5 * g3[:, None, :] * swiglu(h, ffn2_w_in[l], ffn2_w_out[l])
    return rms_norm(x, final_norm)
```

```python
import contextlib
import numpy as np
import concourse.bass as bass
import concourse.mybir as mybir
from concourse.bass_utils import run_bass_kernel_spmd

F32 = mybir.dt.float32
BF16 = mybir.dt.bfloat16
I32 = mybir.dt.int32
AF = mybir.ActivationFunctionType
ALU = mybir.AluOpType
AX = mybir.AxisListType

D = 1024
S = 2048
DEPTH = 4
DFF = 2816
NFC = DFF // 128
KC = D // 128
NTB = S // 512
NCH = S // 128
PROJ = 2884
TOPK = 256
NIT = 16
EPS = 1e-6
ROPE_THETA = 500000.0
NEG = -1.0e30


class Src:
    def __init__(self, nc, name, eng=None):
        self.name = name
        self.eng = eng
        self.sem = nc.alloc_semaphore(name=name)
        self.cnt = 0
        self.seen = {}


class Tl:
    def __init__(self, name=""):
        self.name = name
        self.w = None
        self.rd = {}


class K:
    def __init__(self, n_layers, final_norm, first=True, debug_stop=None):
        self.n_layers = n_layers
        self.final_norm = final_norm
        self.debug_stop = debug_stop
        nc = bass.Bass("TRN2", target_bir_lowering=False)
        self.nc = nc
        self.PE = Src(nc, "s_pe", nc.tensor)
        self.ACT = Src(nc, "s_act", nc.scalar)
        self.DVE = Src(nc, "s_dve", nc.vector)
        self.POOL = Src(nc, "s_pool", nc.gpsimd)
        self.SP = Src(nc, "s_sp", nc.sync)
        self.engs = [self.PE, self.ACT, self.DVE, self.POOL, self.SP]
        self.dsems = []
        self.pe_pending = False
        self.uid = 0

    def sbt(self, name, shape, dt):
        self.uid += 1
        return self.nc.sbuf_tensor(f"{name}_{self.uid}", shape, dt)

    def pst(self, name, shape, dt):
        self.uid += 1
        return self.nc.psum_tensor(f"{name}_{self.uid}", shape, dt)

    def _wait(self, E, deps):
        for src, v in deps.items():
            if src is E and E is self.PE:
                continue
            if E.seen.get(src, 0) < v:
                E.eng.wait_ge(src.sem, v)
                E.seen[src] = v

    def _deps(self, reads, writes):
        deps = {}

        def add(s, v):
            if deps.get(s, 0) < v:
                deps[s] = v
        for t in reads:
            if t.w is not None:
                add(*t.w)
        for t in writes:
            if t.w is not None:
                add(*t.w)
            for s, v in t.rd.items():
                add(s, v)
        return deps

    def op(self, E, reads, writes, emit, inc=True):
        px = [t for t in reads if t.name.startswith("p")]
        if px:
            reads = [t for t in reads if not t.name.startswith("p")]
            writes = list(writes) + px
        self._wait(E, self._deps(reads, writes))
        ins = emit()
        if inc:
            E.cnt += 1
            ins.then_inc(E.sem, 1)
            val = E.cnt
            if E is self.PE:
                self.pe_pending = False
        else:
            assert E is self.PE
            val = E.cnt + 1
            self.pe_pending = True
        for t in reads:
            if t.rd.get(E, 0) < val:
                t.rd[E] = val
        for t in writes:
            t.w = (E, val)
            t.rd = {}
        return ins

    def new_dsem(self, name):
        d = Src(self.nc, name)
        self.dsems.append(d)
        return d

    def dma(self, Q, dsem, reads, writes, emit):
        self._wait(Q, self._deps(reads, writes))
        ins = emit()
        dsem.cnt += 16
        ins.then_inc(dsem.sem, 16)
        for t in reads:
            t.rd[dsem] = dsem.cnt
        for t in writes:
            t.w = (dsem, dsem.cnt)
            t.rd = {}

    def barrier(self):
        assert not self.pe_pending
        srcs = self.engs + self.dsems
        for E in self.engs:
            for s2 in srcs:
                if s2 is E:
                    continue
                if s2.cnt > 0 and E.seen.get(s2, 0) < s2.cnt:
                    E.eng.wait_ge(s2.sem, s2.cnt)
                    E.seen[s2] = s2.cnt

    def mm_group(self, out_ap, out_tl, pairs, reads, inc=True):
        n = len(pairs)
        for i, (l, r) in enumerate(pairs):
            last = (i == n - 1)
            self.op(self.PE, reads, [out_tl],
                    lambda l=l, r=r, i=i, last=last: self.nc.tensor.matmul(out_ap, l, r, start=(i == 0), stop=last),
                    inc=(last and inc))

    def build(self):
        nc = self.nc
        L = self.n_layers
        dr = {}

        def din(name, shape, dt=F32):
            dr[name] = nc.dram_tensor(name, shape, dt, kind="ExternalInput").ap()
            return dr[name]
        din("xT", [D, S])
        din("cT", [128, KC])
        din("pos", [128, NCH], I32)
        din("ada_w", [L, D, 9 * D])
        din("ada_bT", [L, 128, 72])
        din("ng1", [L, 128, KC])
        din("ngm", [L, 128, KC])
        din("ng2", [L, 128, KC])
        din("w1i", [L, NFC, 128, KC * 256])
        din("w1o", [L, DFF, D])
        din("w2i", [L, NFC, 128, KC * 256])
        din("w2o", [L, DFF, D])
        din("wmi", [L, D, PROJ])
        din("wmo", [L, D, D])
        din("sguwT", [L, 4, 128, 128])
        din("sgubT", [L, 128, 4])
        din("fnT", [128, KC])
        self.dr = dr
        self.yT = nc.dram_tensor("yT", [D, S], F32, kind="ExternalOutput").ap()

        with contextlib.ExitStack() as es:
            def sb(name, shape, dt):
                return es.enter_context(self.sbt(name, shape, dt))
            self.xT = sb("xT_sb", [128, KC, S], F32)
            self.hT = sb("hT_sb", [128, KC, S], BF16)
            self.x_tl = [[Tl(f"x{o}_{tb}") for tb in range(NTB)] for o in range(KC)]
            self.h_tl = [Tl(f"h{j}") for j in range(NCH)]
            self.ident = sb("ident", [128, 128], BF16)
            self.ones = sb("ones", [128, 128], BF16)
            self.caus_qs = sb("caus_qs", [128, 128], BF16)
            self.negm = sb("negm", [128, 128], F32)
            self.caus_sq = sb("caus_sq", [128, 128], F32)
            self.p2tab = sb("p2tab", [128, NIT], F32)
            self.cosT = sb("cosT", [128, NCH, 1, 8], F32)
            self.sinT = sb("sinT", [128, NCH, 1, 8], F32)
            self.modT = sb("modT", [128, 72], F32)
            self.cb = sb("cb", [128, KC], BF16)
            self.cvec = sb("cvec", [128, 16], F32)
            self.ng = sb("ng", [128, 3, L, KC], F32)
            self.fn = sb("fn_sb", [128, KC], F32)
            self.avec = sb("avec", [128, KC], F32)
            self.bvec = sb("bvec", [128, KC], F32)
            self.gvec = sb("gvec", [128, KC], F32)
            self.adab = sb("adab", [128, 72], F32)
            self.tl_const = Tl("const")
            self.tl_mod = Tl("mod")
            self.tl_ab = Tl("ab")
            self.tl_g = Tl("g")
            self.tl_adab = Tl("adab")
            self.ld_sem = self.new_dsem("d_ld")
            self.misc_sem = self.new_dsem("d_misc")
            self.st_sems = [self.new_dsem(f"d_st{i}") for i in range(4)]
            self.w_sems = [self.new_dsem(f"d_w{i}") for i in range(4)]

            self.setup()
            for l in range(L):
                self.barrier()
                self.ada(l)
                self.barrier()
                self.ffn(l, 0)
                if self.debug_stop == "ffn1":
                    break
                self.barrier()
                self.mixer(l)
                if self.debug_stop in ("mix", "m1"):
                    break
                self.barrier()
                self.ffn(l, 1)
            self.barrier()
            self.finish()
            self.barrier()
        return nc

    def setup(self):
        nc = self.nc
        dr = self.dr
        L = self.n_layers
        xv = dr["xT"].rearrange("(c p) t -> p c t", p=128)
        for o in range(KC):
            self.dma(self.SP, self.ld_sem, [], [self.x_tl[o][tb] for tb in range(NTB)],
                     lambda o=o: nc.sync.dma_start(out=self.xT[:, o, :], in_=xv[:, o, :]))
        for o in range(KC):
            for tb in range(NTB):
                self.x_tl[o][tb].w = (self.ld_sem, self.ld_sem.cnt)
        with contextlib.ExitStack() as es:
            cf = es.enter_context(self.sbt("cf", [128, KC], F32))
            posi = es.enter_context(self.sbt("posi", [128, NCH], I32))
            posf = es.enter_context(self.sbt("posf", [128, NCH], F32))
            idx = es.enter_context(self.sbt("idx", [128, 128], F32))
            ang = es.enter_context(self.sbt("ang", [128, NCH, 8], F32))
            fr = es.enter_context(self.sbt("fr", [128, NCH, 8], F32))
            fi = es.enter_context(self.sbt("fi", [128, NCH, 8], I32))
            ff = es.enter_context(self.sbt("ff", [128, NCH, 8], F32))
            t_in = Tl("setup_in")
            t_a = Tl("setup_a")
            t_b = Tl("setup_b")
            t_c = Tl("setup_c")
            self.dma(self.SP, self.misc_sem, [], [t_in], lambda: nc.sync.dma_start(out=cf[:], in_=dr["cT"]))
            self.dma(self.SP, self.misc_sem, [], [t_in], lambda: nc.sync.dma_start(out=posi[:], in_=dr["pos"]))
            self.dma(self.SP, self.misc_sem, [], [t_in], lambda: nc.sync.dma_start(out=self.fn[:], in_=dr["fnT"]))
            for i, nm in enumerate(["ng1", "ngm", "ng2"]):
                for l in range(L):
                    self.dma(self.SP, self.misc_sem, [], [t_in],
                             lambda i=i, nm=nm, l=l: nc.sync.dma_start(out=self.ng[:, i, l, :], in_=dr[nm][l]))
            self.op(self.POOL, [], [t_a], lambda: nc.gpsimd.iota(idx[:], [[1, 128]], base=0, channel_multiplier=-1,
                                                                 allow_small_or_imprecise_dtypes=True))
            C = self.tl_const
            self.op(self.DVE, [t_a], [C], lambda: nc.vector.tensor_single_scalar(self.ident[:], idx[:], 0.0, ALU.is_equal))
            self.op(self.DVE, [t_a], [C], lambda: nc.vector.tensor_single_scalar(self.caus_qs[:], idx[:], 0.0, ALU.is_le))
            self.op(self.DVE, [t_a], [C], lambda: nc.vector.tensor_scalar(self.negm[:], idx[:], 0.0, NEG, ALU.is_gt, ALU.mult))
            self.op(self.DVE, [t_a], [C], lambda: nc.vector.tensor_single_scalar(self.caus_sq[:], idx[:], 0.0, ALU.is_ge))
            self.op(self.DVE, [], [C], lambda: nc.vector.memset(self.ones[:], 1.0))
            for it in range(1, NIT + 1):
                self.op(self.DVE, [], [C], lambda it=it: nc.vector.memset(self.p2tab[:, it - 1:it], 2.0 ** (1 - it)))
            self.op(self.DVE, [], [C], lambda: nc.vector.memset(self.cvec[:, 0:1], EPS))
            self.op(self.DVE, [], [C], lambda: nc.vector.memset(self.cvec[:, 1:2], 0.0))
            self.op(self.ACT, [t_in], [C], lambda: nc.scalar.activation(self.cb[:], cf[:], AF.Silu))
            self.op(self.DVE, [t_in], [t_b], lambda: nc.vector.tensor_copy(posf[:], posi[:]))
            for i in range(8):
                inv = float(np.float32(ROPE_THETA) ** np.float32(-(2.0 * i) / 16.0))
                self.op(self.DVE, [t_b], [t_c],
                        lambda i=i, inv=inv: nc.vector.tensor_scalar(ang[:, :, i], posf[:], inv, None, ALU.mult))
            two_pi = 2.0 * np.pi
            for which, dst in ((0, self.sinT), (1, self.cosT)):
                off = 0.0 if which == 0 else 0.25
                self.op(self.DVE, [t_c], [t_b],
                        lambda off=off: nc.vector.tensor_scalar(fr[:], ang[:], 1.0 / two_pi, off, ALU.mult, ALU.add))
                self.op(self.DVE, [t_b], [t_a], lambda: nc.vector.tensor_copy(fi[:], fr[:]))
                self.op(self.DVE, [t_a], [t_a], lambda: nc.vector.tensor_copy(ff[:], fi[:]))
                self.op(self.DVE, [t_a, t_b], [t_b], lambda: nc.vector.tensor_tensor(fr[:], fr[:], ff[:], ALU.subtract))
                self.op(self.DVE, [t_b], [t_a], lambda: nc.vector.tensor_single_scalar(ff[:], fr[:], 0.5, ALU.is_gt))
                self.op(self.DVE, [t_a, t_b], [t_b], lambda: nc.vector.tensor_tensor(fr[:], fr[:], ff[:], ALU.subtract))
                self.op(self.DVE, [t_b], [t_a], lambda: nc.vector.tensor_single_scalar(ff[:], fr[:], -0.5, ALU.is_lt))
                self.op(self.DVE, [t_a, t_b], [t_b], lambda: nc.vector.tensor_tensor(fr[:], fr[:], ff[:], ALU.add))
                self.op(self.ACT, [t_b], [C],
                        lambda dst=dst: nc.scalar.activation(dst[:, :, 0, :], fr[:], AF.Sin, scale=two_pi))
            self.barrier()

    def ada(self, l):
        nc = self.nc
        dr = self.dr
        with contextlib.ExitStack() as es:
            bufs = [es.enter_context(self.sbt(f"adaw{i}", [128, KC, D], BF16)) for i in range(2)]
            btl = [Tl(f"adaw{i}") for i in range(2)]
            ps = es.enter_context(self.pst("ps_mod", [128, 512], F32))
            ps_tl = Tl("ps_mod")
            self.dma(self.SP, self.misc_sem, [], [self.tl_adab],
                     lambda: nc.sync.dma_start(out=self.adab[:], in_=dr["ada_bT"][l]))
            wv = dr["ada_w"][l].rearrange("(kc p) n -> p kc n", p=128)
            for v in range(9):
                b = v % 2
                self.dma(self.POOL, self.w_sems[b], [], [btl[b]],
                         lambda v=v, b=b: nc.gpsimd.dma_start(out=bufs[b][:], in_=wv[:, :, v * D:(v + 1) * D]))
                for jc in range(KC):
                    col = v * KC + jc
                    self.mm_group(ps[:, col:col + 1], ps_tl,
                                  [(bufs[b][:, kc, jc * 128:(jc + 1) * 128], self.cb[:, kc:kc + 1]) for kc in range(KC)],
                                  [btl[b], self.tl_const], inc=(jc == KC - 1))
            self.op(self.DVE, [ps_tl, self.tl_adab], [self.tl_mod],
                    lambda: nc.vector.tensor_tensor(self.modT[:], ps[:, 0:72], self.adab[:], ALU.add))
            self.barrier()

    def make_h(self, es, ps_n, ps_n_tl, g_ap, i_shift, i_scale, rstd=None, rstd_tl=None):
        nc = self.nc
        sq = [es.enter_context(self.sbt(f"sq{i}", [128, 512], BF16)) for i in range(2)]
        sq_tl = [Tl("sq0"), Tl("sq1")]
        if rstd is None:
            rstd = es.enter_context(self.sbt("rstd", [128, 512], F32))
            rstd_tl = Tl("rstd")
        t32 = [es.enter_context(self.sbt(f"t32_{i}", [128, 512], F32)) for i in range(2)]
        t32_tl = [Tl("t32a"), Tl("t32b")]
        M = self.modT
        self.op(self.DVE, [self.tl_mod, self.tl_const], [self.tl_ab],
                lambda: nc.vector.scalar_tensor_tensor(self.avec[:], M[:, i_scale * KC:(i_scale + 1) * KC], 1.0, g_ap,
                                                       ALU.add, ALU.mult))
        self.op(self.DVE, [self.tl_mod], [self.tl_ab],
                lambda: nc.vector.tensor_copy(self.bvec[:], M[:, i_shift * KC:(i_shift + 1) * KC]))
        for tb in range(NTB):
            ts = slice(tb * 512, (tb + 1) * 512)
            for kc in range(KC):
                b = kc % 2
                self.op(self.ACT, [self.x_tl[kc][tb]], [sq_tl[b]],
                        lambda kc=kc, b=b, ts=ts: nc.scalar.activation(sq[b][:], self.xT[:, kc, ts], AF.Square))
                self.op(self.PE, [sq_tl[b], self.tl_const], [ps_n_tl],
                        lambda kc=kc, b=b: nc.tensor.matmul(ps_n[:], self.ones[:], sq[b][:], start=(kc == 0), stop=(kc == KC - 1)),
                        inc=True)
            self.op(self.ACT, [ps_n_tl, self.tl_const], [rstd_tl],
                    lambda: nc.scalar.activation(rstd[:], ps_n[:], AF.Sqrt, bias=self.cvec[:, 0:1], scale=1.0 / D))
            self.op(self.DVE, [rstd_tl], [rstd_tl], lambda: nc.vector.reciprocal(rstd[:], rstd[:]))
            for kc in range(KC):
                b = kc % 2
                self.op(self.DVE, [self.x_tl[kc][tb], rstd_tl, self.tl_ab], [t32_tl[b]],
                        lambda kc=kc, b=b, ts=ts: nc.vector.scalar_tensor_tensor(t32[b][:], self.xT[:, kc, ts], self.avec[:, kc:kc + 1],
                                                                                 rstd[:], ALU.mult, ALU.mult))
                self.op(self.ACT, [t32_tl[b], self.tl_ab], [self.h_tl[4 * tb + i] for i in range(4)],
                        lambda kc=kc, b=b, ts=ts: nc.scalar.activation(self.hT[:, kc, ts], t32[b][:], AF.Identity,
                                                                       bias=self.bvec[:, kc:kc + 1], scale=1.0))

    def ffn(self, l, which):
        nc = self.nc
        dr = self.dr
        w_in = dr["w1i" if which == 0 else "w2i"][l]
        w_out = dr["w1o" if which == 0 else "w2o"][l]
        i_sh, i_sc, i_g = (0, 1, 2) if which == 0 else (6, 7, 8)
        g_ap = self.ng[:, 0 if which == 0 else 2, l, :]
        NP = 11
        with contextlib.ExitStack() as es:
            act = es.enter_context(self.sbt("act", [128, NP, S], BF16))
            act_tl = [[Tl(f"act{c}_{tb}") for tb in range(NTB)] for c in range(NP)]
            wout = es.enter_context(self.sbt("wout", [128, NP, D], BF16))
            wout_tl = Tl("wout")
            NB = 3
            win = [es.enter_context(self.sbt(f"win{i}", [128, KC, 256], BF16)) for i in range(NB)]
            win_tl = [Tl(f"win{i}") for i in range(NB)]
            sg = [es.enter_context(self.sbt(f"sg{i}", [128, 512], BF16)) for i in range(2)]
            sg_tl = [Tl("sg0"), Tl("sg1")]
            ps_g = [es.enter_context(self.pst(f"ps_g{i}", [128, 512], F32)) for i in range(2)]
            ps_u = [es.enter_context(self.pst(f"ps_u{i}", [128, 512], F32)) for i in range(2)]
            ps_o = [es.enter_context(self.pst(f"ps_o{i}", [128, 512], F32)) for i in range(2)]
            ps_n = es.enter_context(self.pst("ps_n", [128, 512], F32))
            psg_tl = [Tl("psg0"), Tl("psg1")]
            psu_tl = [Tl("psu0"), Tl("psu1")]
            pso_tl = [Tl("pso0"), Tl("pso1")]
            psn_tl = Tl("psn")

            def load_win(c_glob):
                b = c_glob % NB
                self.dma(self.POOL, self.w_sems[b], [], [win_tl[b]],
                         lambda: nc.gpsimd.dma_start(out=win[b][:], in_=w_in[c_glob].rearrange("p (kc n) -> p kc n", kc=KC)))

            load_win(0)
            load_win(1)
            self.op(self.DVE, [self.tl_mod], [self.tl_g],
                    lambda: nc.vector.tensor_scalar(self.gvec[:], self.modT[:, i_g * KC:(i_g + 1) * KC], 0.5, None, ALU.mult))
            self.make_h(es, ps_n, psn_tl, g_ap, i_sh, i_sc)
            cnt = 0
            ocnt = 0
            for part in range(2):
                self.dma(self.POOL, self.w_sems[3], [], [wout_tl],
                         lambda part=part: nc.gpsimd.dma_start(
                             out=wout[:], in_=w_out[part * NP * 128:(part + 1) * NP * 128, :].rearrange("(c p) n -> p c n", p=128)))
                for ci in range(NP):
                    c = part * NP + ci
                    if c + 2 < NFC:
                        load_win(c + 2)
                    b = c % NB
                    for tb in range(NTB):
                        ts = slice(tb * 512, (tb + 1) * 512)
                        pb = cnt % 2
                        cnt += 1
                        hts = [self.h_tl[4 * tb + i] for i in range(4)]
                        self.mm_group(ps_g[pb][:], psg_tl[pb],
                                      [(win[b][:, kc, 0:128], self.hT[:, kc, ts]) for kc in range(KC)], [win_tl[b]] + hts)
                        self.mm_group(ps_u[pb][:], psu_tl[pb],
                                      [(win[b][:, kc, 128:256], self.hT[:, kc, ts]) for kc in range(KC)], [win_tl[b]] + hts)
                        self.op(self.ACT, [psg_tl[pb]], [sg_tl[pb]],
                                lambda pb=pb: nc.scalar.activation(sg[pb][:], ps_g[pb][:], AF.Silu))
                        self.op(self.DVE, [sg_tl[pb], psu_tl[pb]], [act_tl[ci][tb]],
                                lambda pb=pb, ci=ci, ts=ts: nc.vector.tensor_tensor(act[:, ci, ts], sg[pb][:], ps_u[pb][:], ALU.mult))
                for tb in range(NTB):
                    ts = slice(tb * 512, (tb + 1) * 512)
                    for o in range(KC):
                        pb = ocnt % 2
                        ocnt += 1
                        self.mm_group(ps_o[pb][:], pso_tl[pb],
                                      [(wout[:, ci, o * 128:(o + 1) * 128], act[:, ci, ts]) for ci in range(NP)],
                                      [wout_tl] + [act_tl[ci][tb] for ci in range(NP)])
                        self.op(self.DVE, [pso_tl[pb], self.tl_g, self.x_tl[o][tb]], [self.x_tl[o][tb]],
                                lambda pb=pb, o=o, ts=ts: nc.vector.scalar_tensor_tensor(
                                    self.xT[:, o, ts], ps_o[pb][:], self.gvec[:, o:o + 1], self.xT[:, o, ts], ALU.mult, ALU.add))
            self.barrier()

    def transposes(self, pT, pT_tl, src_ap_fn, src_tl, n):
        nc = self.nc
        for c in range(n):
            self.op(self.PE, [src_tl, self.tl_const], [pT_tl],
                    lambda c=c: nc.tensor.transpose(pT[:, c * 128:(c + 1) * 128], src_ap_fn(c), self.ident[:]),
                    inc=(c == n - 1))

    def rope(self, ps3, dst3, j, H, tmp, tmp_tl, ps_tl, dst_tl):
        nc = self.nc
        cos = self.cosT[:, j, :, :].broadcast_to([128, H, 8])
        sin = self.sinT[:, j, :, :].broadcast_to([128, H, 8])
        x1 = ps3[:, :, 0:8]
        x2 = ps3[:, :, 8:16]
        t1, t2, t3, t4 = [t[:, 0:H, :] for t in tmp]
        C = self.tl_const
        self.op(self.DVE, [ps_tl, C], [tmp_tl[0]], lambda: nc.vector.tensor_tensor(t1, x1, cos, ALU.mult))
        self.op(self.DVE, [ps_tl, C], [tmp_tl[1]], lambda: nc.vector.tensor_tensor(t2, x2, sin, ALU.mult))
        self.op(self.DVE, [tmp_tl[0], tmp_tl[1]], [dst_tl], lambda: nc.vector.tensor_tensor(dst3[:, :, 0:8], t1, t2, ALU.subtract))
        self.op(self.DVE, [ps_tl, C], [tmp_tl[2]], lambda: nc.vector.tensor_tensor(t3, x2, cos, ALU.mult))
        self.op(self.DVE, [ps_tl, C], [tmp_tl[3]], lambda: nc.vector.tensor_tensor(t4, x1, sin, ALU.mult))
        self.op(self.DVE, [tmp_tl[2], tmp_tl[3]], [dst_tl], lambda: nc.vector.tensor_tensor(dst3[:, :, 8:16], t3, t4, ALU.add))
        self.op(self.ACT, [ps_tl], [dst_tl], lambda: nc.scalar.copy(dst3[:, :, 16:64], ps3[:, :, 16:64]))

    def mixer(self, l):
        nc = self.nc
        dr = self.dr
        wmi = dr["wmi"][l].rearrange("(kc p) n -> p kc n", p=128)
        C = self.tl_const
        with contextlib.ExitStack() as es:
            def sb(name, shape, dt):
                return es.enter_context(self.sbt(name, shape, dt))
            qT = sb("qT", [128, 4, S], BF16)
            kT = sb("kT", [128, 4, S], BF16)
            Vx = sb("Vx", [128, NCH, 8, 65], BF16)
            qiT = sb("qiT", [128, 2, S], BF16)
            kiT2 = sb("kiT2", [128, S], BF16)
            wi_all = sb("wi_all", [128, NCH, 4], F32)
            q_tl = [Tl(f"q{j}") for j in range(NCH)]
            k_tl = [Tl(f"k{j}") for j in range(NCH)]
            v_tl = [Tl(f"v{j}") for j in range(NCH)]
            qi_tl = [Tl(f"qi{j}") for j in range(NCH)]
            ki_tl = [Tl(f"ki{j}") for j in range(NCH)]
            wi_tl = [Tl(f"wi{j}") for j in range(NCH)]
            self.op(self.DVE, [self.tl_mod], [self.tl_g],
                    lambda: nc.vector.tensor_copy(self.gvec[:], self.modT[:, 5 * KC:6 * KC]))
            self.op(self.DVE, [], v_tl, lambda: nc.vector.memset(Vx[:, :, :, 64:65], 1.0))

            with contextlib.ExitStack() as e1:
                def sb1(name, shape, dt):
                    return e1.enter_context(self.sbt(name, shape, dt))
                wblk = [sb1(f"wblk{i}", [128, KC, 512], BF16) for i in range(2)]
                wblk_tl = [Tl("wblk0"), Tl("wblk1")]
                va_tok = sb1("va_tok", [128, NCH, 512], BF16)
                va_tl = [Tl(f"va{j}") for j in range(NCH)]
                tok = [sb1(f"tok{i}", [128, 512], BF16) for i in range(2)]
                tok_tl = [Tl("tok0"), Tl("tok1")]
                u_tok = [sb1(f"utok{i}", [128, 512], BF16) for i in range(2)]
                u_tl = [Tl("u0"), Tl("u1")]
                rt = [sb1(f"rt{i}", [128, 8, 8], F32) for i in range(4)]
                rt_tl = [Tl(f"rt{i}") for i in range(4)]
                rstd_m = sb1("rstd_m", [128, 512], F32)
                sguw32 = rstd_m[:, :].rearrange("p (g t) -> p g t", t=128)
                sguw = sb1("sguw", [128, 4, 128], BF16)
                sgub = sb1("sgub", [128, 4], F32)
                sg_tl = Tl("sguw")
                sg32_tl = Tl("sguw32")
                sgb_tl = Tl("sgub")
                pp = [e1.enter_context(self.pst(f"pp{i}", [128, 512], F32)) for i in range(2)]
                pp_tl = [Tl("pp0"), Tl("pp1")]
                pT = [e1.enter_context(self.pst(f"pT{i}", [128, 1024], BF16)) for i in range(2)]
                pT_tl = [Tl("pT0"), Tl("pT1")]
                p_sgu = e1.enter_context(self.pst("p_sgu", [128, 512], F32))
                psgu_tl = Tl("psgu")
                ps_n = e1.enter_context(self.pst("ps_nm", [128, 512], F32))
                psn_tl = Tl("psnm")

                blocks = [("q", 1024, 512), ("k", 1536, 512), ("v", 2048, 512), ("idx", 2560, 324), ("va", 512, 512), ("u", 0, 512)]

                def load_blk(bi):
                    nm, c0, ncol = blocks[bi]
                    b = bi % 2
                    self.dma(self.POOL, self.w_sems[b], [], [wblk_tl[b]],
                             lambda: nc.gpsimd.dma_start(out=wblk[b][:, :, 0:ncol], in_=wmi[:, :, c0:c0 + ncol]))
                load_blk(0)
                self.dma(self.SP, self.misc_sem, [], [sg32_tl],
                         lambda: nc.sync.dma_start(out=sguw32, in_=dr["sguwT"][l].rearrange("g s t -> s g t")))
                self.dma(self.SP, self.misc_sem, [], [sgb_tl],
                         lambda: nc.sync.dma_start(out=sgub[:], in_=dr["sgubT"][l]))
                sg32_tl.w = (self.misc_sem, self.misc_sem.cnt)
                sgb_tl.w = (self.misc_sem, self.misc_sem.cnt)
                for g in range(4):
                    self.op(self.DVE, [sg32_tl, C], [sg_tl],
                            lambda g=g: nc.vector.tensor_tensor(sguw[:, g, :], sguw32[:, g, :], self.caus_sq[:], ALU.mult))
                self.make_h(e1, ps_n, psn_tl, self.ng[:, 1, l, :], 3, 4, rstd=rstd_m, rstd_tl=sg32_tl)
                cnt = 0
                for bi, (nm, c0, ncol) in enumerate(blocks):
                    if bi + 1 < len(blocks):
                        load_blk(bi + 1)
                    b = bi % 2
                    for j in range(NCH):
                        js = slice(j * 128, (j + 1) * 128)
                        pb = cnt % 2
                        cnt += 1
                        ps = pp[pb]
                        self.mm_group(ps[:, 0:ncol], pp_tl[pb],
                                      [(self.hT[:, kc, js], wblk[b][:, kc, 0:ncol]) for kc in range(KC)],
                                      [self.h_tl[j], wblk_tl[b]])
                        if nm in ("q", "k"):
                            dstT, d_tl = (qT, q_tl) if nm == "q" else (kT, k_tl)
                            ps3 = ps[:, 0:512].rearrange("p (h d) -> p h d", d=64)
                            d3 = tok[pb][:, 0:512].rearrange("p (h d) -> p h d", d=64)
                            self.rope(ps3, d3, j, 8, rt, rt_tl, pp_tl[pb], tok_tl[pb])
                            self.transposes(pT[pb], pT_tl[pb], lambda c, pb=pb: tok[pb][:, c * 128:(c + 1) * 128], tok_tl[pb], 4)
                            self.op(self.ACT if j % 2 else self.DVE, [pT_tl[pb]], [d_tl[j]],
                                    (lambda pb=pb, js=js, dstT=dstT: nc.scalar.copy(dstT[:, :, js], pT[pb][:, 0:512].rearrange("p (c t) -> p c t", t=128)))
                                    if j % 2 else
                                    (lambda pb=pb, js=js, dstT=dstT: nc.vector.tensor_copy(dstT[:, :, js], pT[pb][:, 0:512].rearrange("p (c t) -> p c t", t=128))))
                        elif nm == "v":
                            self.op(self.ACT, [pp_tl[pb]], [v_tl[j]],
                                    lambda pb=pb, j=j: nc.scalar.copy(Vx[:, j, :, 0:64], pp[pb][:, 0:512].rearrange("p (h d) -> p h d", d=64)))
                        elif nm == "idx":
                            ps3 = ps[:, 0:256].rearrange("p (h d) -> p h d", d=64)
                            d3 = tok[pb][:, 0:256].rearrange("p (h d) -> p h d", d=64)
                            self.rope(ps3, d3, j, 4, rt, rt_tl, pp_tl[pb], tok_tl[pb])
                            ps3k = ps[:, 256:320].rearrange("p (h d) -> p h d", d=64)
                            d3k = tok[pb][:, 256:320].rearrange("p (h d) -> p h d", d=64)
                            self.rope(ps3k, d3k, j, 1, rt, rt_tl, pp_tl[pb], tok_tl[pb])
                            self.op(self.DVE, [tok_tl[pb]], [tok_tl[pb]],
                                    lambda pb=pb: nc.vector.tensor_copy(tok[pb][:, 320:384], tok[pb][:, 256:320]))
                            self.op(self.DVE, [pp_tl[pb]], [wi_tl[j]],
                                    lambda pb=pb, j=j: nc.vector.tensor_copy(wi_all[:, j, :], pp[pb][:, 320:324]))
                            self.transposes(pT[pb], pT_tl[pb], lambda c, pb=pb: tok[pb][:, c * 128:(c + 1) * 128], tok_tl[pb], 3)
                            self.op(self.ACT, [pT_tl[pb]], [qi_tl[j]],
                                    lambda pb=pb, js=js: nc.scalar.copy(qiT[:, :, js], pT[pb][:, 0:256].rearrange("p (c t) -> p c t", t=128)))
                            self.op(self.DVE, [pT_tl[pb]], [ki_tl[j]],
                                    lambda pb=pb, js=js: nc.vector.tensor_copy(kiT2[:, js], pT[pb][:, 256:384]))
                        elif nm == "va":
                            self.op(self.ACT, [pp_tl[pb]], [va_tl[j]],
                                    lambda pb=pb, j=j: nc.scalar.activation(va_tok[:, j, :], pp[pb][:, 0:512], AF.Gelu))
                        else:
                            self.op(self.ACT, [pp_tl[pb]], [u_tl[pb]],
                                    lambda pb=pb: nc.scalar.activation(u_tok[pb][:], pp[pb][:, 0:512], AF.Gelu))
                            for g in range(4):
                                gs = slice(g * 128, (g + 1) * 128)
                                self.op(self.PE, [sg_tl, va_tl[j]], [psgu_tl],
                                        lambda g=g, gs=gs, j=j: nc.tensor.matmul(p_sgu[:, gs], sguw[:, g, :], va_tok[:, j, gs], start=True, stop=True),
                                        inc=(g == 3))
                            for g in range(4):
                                gs = slice(g * 128, (g + 1) * 128)
                                self.op(self.DVE, [psgu_tl, u_tl[pb], sgb_tl], [tok_tl[pb]],
                                        lambda g=g, gs=gs, pb=pb: nc.vector.scalar_tensor_tensor(
                                            tok[pb][:, gs], p_sgu[:, gs], sgub[:, g:g + 1], u_tok[pb][:, gs], ALU.add, ALU.mult))
                            self.transposes(pT[pb], pT_tl[pb], lambda c, pb=pb: tok[pb][:, c * 128:(c + 1) * 128], tok_tl[pb], 4)
                            self.op(self.ACT, [pT_tl[pb]], [self.h_tl[j]],
                                    lambda pb=pb, js=js: nc.scalar.copy(self.hT[:, 0:4, js], pT[pb][:, 0:512].rearrange("p (c t) -> p c t", t=128)))
                self.barrier()

            with contextlib.ExitStack() as e2:
              if self.debug_stop != "m1":
                def sb2(name, shape, dt):
                    return e2.enter_context(self.sbt(name, shape, dt))
                score = [sb2(f"score{i}", [128, S], F32) for i in range(3)]
                score_tl = [Tl("score0"), Tl("score1"), Tl("score2")]
                junk = sb2("junk", [128, S], mybir.dt.int8)
                junk_tl = Tl("junk")
                junkA = sb2("junkA", [128, S], mybir.dt.int8)
                junkA_tl = Tl("junkA")
                cN = sb2("cN", [128, NCH], F32)
                for jj in range(NCH):
                    self.op(self.DVE, [], [C], lambda jj=jj: nc.vector.memset(cN[:, jj:jj + 1], float((jj + 1) * 128 - (2 * TOPK - 1))))
                mask = sb2("mask", [128, S], BF16)
                mask_tl = Tl("mask")
                maskT = [sb2(f"maskT{i}", [128, NCH, 128], BF16) for i in range(2)]
                maskT_tl = [Tl("maskT0"), Tl("maskT1")]
                Rb = [sb2(f"Rb{i}", [128, 512], F32) for i in range(2)]
                Rb_tl = [Tl("Rb0"), Tl("Rb1")]
                NPT = 2
                PT = [sb2(f"PT{i}", [128, 512], BF16) for i in range(NPT)]
                PT_tl = [Tl(f"PT{i}") for i in range(NPT)]
                b_tok = [sb2("btok0", [128, 512], BF16)] * 2
                b_tl = [Tl("btok0")] * 2
                sm = sb2("sm", [128, 3, 8], F32)
                sm_tl = [Tl("sm0"), Tl("sm1"), Tl("sm2")]
                wtab = sb2("wtab", [128, 3, NIT], F32)
                rden = sb2("rden", [128, 8], F32)
                rden_tl = Tl("rden")
                pL = [e2.enter_context(self.pst(f"pL{i}", [128, 512], F32)) for i in range(2)]
                pL_tl = [Tl("pL0"), Tl("pL1")]
                pS = [e2.enter_context(self.pst(f"pS{i}", [128, 512], F32)) for i in range(2)]
                pS_tl = [Tl("pS0"), Tl("pS1")]
                pO = [e2.enter_context(self.pst(f"pO{i}", [128, 512], F32)) for i in range(2)]
                pO_tl = [Tl("pO0"), Tl("pO1")]
                pT = [e2.enter_context(self.pst(f"pT2_{i}", [128, 1024], BF16)) for i in range(2)]
                pT_tl = [Tl("pT2_0"), Tl("pT2_1")]
                for p in range(3):
                    self.op(self.DVE, [], [sm_tl[p]], lambda p=p: nc.vector.memset(sm[:, p, 5:6], -1.0e29))
                st = {"scnt": 0, "tcnt": 0}

                def indexer(j):
                    units = []
                    p = j % 3
                    js = slice(j * 128, (j + 1) * 128)
                    n_s = (j + 1) * 128
                    nblk = (n_s + 511) // 512
                    for sbk in range(nblk):
                        w = min(512, n_s - sbk * 512)
                        ss = slice(sbk * 512, sbk * 512 + w)
                        for h in range(4):
                            units.append(lambda sbk=sbk, w=w, ss=ss, h=h: idx_unit(j, p, js, sbk, w, ss, h))
                    return units

                def idx_unit(j, p, js, sbk, w, ss, h):
                        if True:
                            hp, e = h // 2, h % 2
                            ps_ = slice(e * 64, (e + 1) * 64)
                            self.op(self.PE, [qi_tl[j]] + ki_tl[sbk * 4:sbk * 4 + (w // 128)], [pL_tl[e]],
                                    lambda e=e, hp=hp, ps_=ps_, ss=ss, w=w: nc.tensor.matmul(
                                        pL[e][:, 0:w], qiT[ps_, hp, js], kiT2[ps_, ss], start=True, stop=True))
                            self.op(self.ACT, [pL_tl[e]], [Rb_tl[e]],
                                    lambda e=e, w=w: nc.scalar.activation(Rb[e][:, 0:w], pL[e][:, 0:w], AF.Relu))
                            if h == 0:
                                self.op(self.DVE, [Rb_tl[e], wi_tl[j]], [score_tl[p]],
                                        lambda e=e, w=w, ss=ss: nc.vector.tensor_scalar(
                                            score[p][:, ss], Rb[e][:, 0:w], wi_all[:, j, 0:1], None, ALU.mult))
                            else:
                                self.op(self.DVE, [Rb_tl[e], wi_tl[j], score_tl[p]], [score_tl[p]],
                                        lambda e=e, w=w, ss=ss, h=h: nc.vector.scalar_tensor_tensor(
                                            score[p][:, ss], Rb[e][:, 0:w], wi_all[:, j, h:h + 1], score[p][:, ss], ALU.mult, ALU.add))

                def chain(j):
                    p = j % 3
                    js = slice(j * 128, (j + 1) * 128)
                    n_s = (j + 1) * 128
                    sc = score[p]
                    stl = score_tl[p]
                    mt = sm_tl[p]
                    use_act = (j % 2 == 1)
                    units = []

                    def pre():
                        if j >= 2:
                            self.op(self.DVE, [stl], [mt],
                                    lambda: nc.vector.tensor_reduce(sm[:, p, 0:1], sc[:, 0:n_s], AX.X, ALU.max, apply_absolute_value=True))
                        self.op(self.DVE, [stl, C], [stl],
                                lambda: nc.vector.tensor_tensor(sc[:, js], sc[:, js], self.negm[:], ALU.add))
                        if j >= 2:
                            if not use_act:
                                self.op(self.DVE, [mt, C], [mt],
                                        lambda: nc.vector.tensor_scalar(wtab[:, p, :], self.p2tab[:], sm[:, p, 0:1], None, ALU.mult))
                                self.op(self.DVE, [mt], [mt], lambda: nc.vector.memset(sm[:, p, 1:2], 0.0))
                            else:
                                self.op(self.DVE, [mt, C], [mt],
                                        lambda: nc.vector.tensor_scalar(wtab[:, p, :], self.p2tab[:], sm[:, p, 0:1], -0.5, ALU.mult, ALU.mult))
                                self.op(self.DVE, [mt], [mt], lambda: nc.vector.memset(sm[:, p, 1:2], 0.0))

                    def it_dve(it):
                        self.op(self.DVE, [mt, stl], [mt, junk_tl],
                                lambda: nc.vector.tensor_scalar(junk[:, 0:n_s], sc[:, 0:n_s], sm[:, p, 1:2], 0.0,
                                                                ALU.is_ge, ALU.add, accum_out=sm[:, p, 2:3]))
                        self.op(self.DVE, [mt], [mt],
                                lambda: nc.vector.tensor_scalar(sm[:, p, 3:4], sm[:, p, 2:3], TOPK - 0.5, 0.5, ALU.is_ge, ALU.subtract))
                        self.op(self.DVE, [mt], [mt],
                                lambda: nc.vector.scalar_tensor_tensor(sm[:, p, 1:2], sm[:, p, 3:4], wtab[:, p, it - 1:it],
                                                                       sm[:, p, 1:2], ALU.mult, ALU.add))

                    def it_act(it):
                        src_c = 1 if (it % 2 == 1) else 6
                        dst_c = 6 if (it % 2 == 1) else 1
                        self.op(self.ACT, [mt, stl], [mt, junkA_tl],
                                lambda: nc.scalar.activation(junkA[:, 0:n_s], sc[:, 0:n_s], AF.Sign, bias=sm[:, p, src_c:src_c + 1], scale=1.0,
                                                             accum_out=sm[:, p, 2:3]))
                        self.op(self.ACT, [mt], [mt],
                                lambda: nc.scalar.activation(sm[:, p, 3:4], sm[:, p, 2:3], AF.Sign, bias=cN[:, j:j + 1], scale=1.0))
                        self.op(self.ACT, [mt], [mt],
                                lambda: nc.scalar.activation(sm[:, p, dst_c:dst_c + 1], sm[:, p, 3:4], AF.Identity,
                                                             bias=sm[:, p, src_c:src_c + 1], scale=wtab[:, p, it - 1:it]))

                    if j >= 2:
                        for it in range(1, NIT + 1):
                            units.append((lambda it=it: it_act(it)) if use_act else (lambda it=it: it_dve(it)))

                    def post():
                        if j >= 2:
                            if not use_act:
                                self.op(self.DVE, [mt], [mt],
                                        lambda: nc.vector.scalar_tensor_tensor(sm[:, p, 4:5], sm[:, p, 0:1], -(2.0 ** (-NIT)), sm[:, p, 1:2],
                                                                               ALU.mult, ALU.add))
                                self.op(self.DVE, [mt, stl], [mask_tl],
                                        lambda: nc.vector.tensor_scalar(mask[:, 0:n_s], sc[:, 0:n_s], sm[:, p, 4:5], None, ALU.is_ge))
                            else:
                                fin_c = 6 if (NIT % 2 == 1) else 1
                                self.op(self.DVE, [mt, stl], [mask_tl],
                                        lambda: nc.vector.tensor_scalar(mask[:, 0:n_s], sc[:, 0:n_s], sm[:, p, fin_c:fin_c + 1],
                                                                        wtab[:, p, NIT - 1:NIT], ALU.add, ALU.is_ge))
                        else:
                            self.op(self.DVE, [mt, stl], [mask_tl],
                                    lambda: nc.vector.tensor_scalar(mask[:, 0:n_s], sc[:, 0:n_s], sm[:, p, 5:6], None, ALU.is_ge))
                    return pre, units, post

                def mask_transposes(j):
                    p = j % 2
                    for i0 in range(0, j + 1, 4):
                        n = min(4, j + 1 - i0)
                        tb_ = st["tcnt"] % 2
                        st["tcnt"] += 1
                        self.transposes(pT[tb_], pT_tl[tb_], lambda c, i0=i0: mask[:, (i0 + c) * 128:(i0 + c + 1) * 128], mask_tl, n)
                        self.op(self.ACT, [pT_tl[tb_]], [maskT_tl[p]],
                                lambda tb_=tb_, i0=i0, n=n: nc.scalar.copy(maskT[p][:, i0:i0 + n, :],
                                                                           pT[tb_][:, 0:n * 128].rearrange("p (c t) -> p c t", t=128)))

                def attention(j):
                    units = []
                    for h in range(8):
                        for i0 in range(0, j + 1, 4):
                            units.append(lambda h=h, i0=i0: att_unit(j, h, i0))
                    units.append(lambda: att_tail(j))
                    return units

                def att_unit(j, h, i0):
                    p = j % 2
                    js = slice(j * 128, (j + 1) * 128)
                    if True:
                        c, e = h // 2, h % 2
                        ps_ = slice(e * 64, (e + 1) * 64)
                        ob = h // 4
                        oc = (h % 4) * 65
                        if True:
                            n = min(4, j + 1 - i0)
                            sbf = st["scnt"] % 2
                            pbf = st["scnt"] % NPT
                            st["scnt"] += 1
                            for t in range(n):
                                i = i0 + t
                                self.op(self.PE, [k_tl[i], q_tl[j]], [pS_tl[sbf]],
                                        lambda t=t, i=i, sbf=sbf: nc.tensor.matmul(
                                            pS[sbf][:, t * 128:(t + 1) * 128], kT[ps_, c, i * 128:(i + 1) * 128], qT[ps_, c, js],
                                            start=True, stop=True),
                                        inc=(t == n - 1))
                            self.op(self.ACT, [pS_tl[sbf]], [PT_tl[pbf]],
                                    lambda sbf=sbf, pbf=pbf, n=n: nc.scalar.activation(PT[pbf][:, 0:n * 128], pS[sbf][:, 0:n * 128], AF.Exp, scale=0.125))
                            self.op(self.POOL, [PT_tl[pbf], maskT_tl[p]], [PT_tl[pbf]],
                                    lambda pbf=pbf, n=n, i0=i0: nc.gpsimd.tensor_tensor(
                                        PT[pbf][:, 0:n * 128].rearrange("p (c t) -> p c t", t=128),
                                        PT[pbf][:, 0:n * 128].rearrange("p (c t) -> p c t", t=128),
                                        maskT[p][:, i0:i0 + n, :], ALU.mult))
                            for t in range(n):
                                i = i0 + t
                                self.op(self.PE, [PT_tl[pbf], v_tl[i]], [pO_tl[ob]],
                                        lambda t=t, i=i, pbf=pbf: nc.tensor.matmul(
                                            pO[ob][:, oc:oc + 65], PT[pbf][:, t * 128:(t + 1) * 128], Vx[:, i, h, :],
                                            start=(i == 0), stop=(i == j)),
                                        inc=(t == n - 1))
                def att_tail(j):
                    js = slice(j * 128, (j + 1) * 128)
                    bb = j % 2
                    for ob in range(2):
                        self.op(self.DVE, [pO_tl[ob]], [rden_tl],
                                lambda ob=ob: nc.vector.reciprocal(rden[:, ob * 4:(ob + 1) * 4],
                                                                   pO[ob][:, 0:260].rearrange("p (h d) -> p h d", d=65)[:, :, 64]))
                    for h in range(8):
                        ob = h // 4
                        oc = (h % 4) * 65
                        self.op(self.ACT, [pO_tl[ob], rden_tl], [b_tl[bb]],
                                lambda h=h, ob=ob, oc=oc: nc.scalar.activation(
                                    b_tok[bb][:, h * 64:(h + 1) * 64], pO[ob][:, oc:oc + 64], AF.Copy, scale=rden[:, h:h + 1]))
                    tb_ = st["tcnt"] % 2
                    st["tcnt"] += 1
                    self.transposes(pT[tb_], pT_tl[tb_], lambda c: b_tok[bb][:, c * 128:(c + 1) * 128], b_tl[bb], 4)
                    self.op(self.ACT, [pT_tl[tb_]], [self.h_tl[j]],
                            lambda tb_=tb_: nc.scalar.copy(self.hT[:, 4:8, js], pT[tb_][:, 0:512].rearrange("p (c t) -> p c t", t=128)))

                def run_merged(lists):
                    items = []
                    for li, us in enumerate(lists):
                        for k, u in enumerate(us):
                            items.append(((k + 0.5) / len(us), li, k, u))
                    items.sort(key=lambda x: (x[0], x[1], x[2]))
                    for _, _, _, u in items:
                        u()

                for jj in (0, 1):
                    for u in indexer(jj):
                        u()
                chains = {}
                chains[0] = chain(0)
                chains[0][0]()
                for u in chains[0][1]:
                    u()
                for j in range(NCH + 1):
                    lists = []
                    if j + 1 < NCH:
                        chains[j + 1] = chain(j + 1)
                        chains[j + 1][0]()
                        lists.append(chains[j + 1][1])
                    if j >= 1:
                        lists.append(attention(j - 1))
                    if j + 2 < NCH:
                        lists.append(indexer(j + 2))
                    run_merged([l for l in lists if l])
                    if j < NCH:
                        chains[j][2]()
                        mask_transposes(j)
                self.barrier()

            with contextlib.ExitStack() as e3:
                wmo = e3.enter_context(self.sbt("wmo_sb", [128, KC, D], BF16))
                wmo_tl = Tl("wmo")
                po = [e3.enter_context(self.pst(f"po{i}", [128, 512], F32)) for i in range(2)]
                po_tl = [Tl("po0"), Tl("po1")]
                self.dma(self.POOL, self.w_sems[0], [], [wmo_tl],
                         lambda: nc.gpsimd.dma_start(out=wmo[:], in_=dr["wmo"][l].rearrange("(kc p) n -> p kc n", p=128)))
                ocnt = 0
                for tb in range(NTB):
                    ts = slice(tb * 512, (tb + 1) * 512)
                    for o in range(KC):
                        pb = ocnt % 2
                        ocnt += 1
                        self.mm_group(po[pb][:], po_tl[pb],
                                      [(wmo[:, kc, o * 128:(o + 1) * 128], self.hT[:, kc, ts]) for kc in range(KC)],
                                      [wmo_tl] + [self.h_tl[4 * tb + i] for i in range(4)])
                        self.op(self.DVE, [po_tl[pb], self.tl_g, self.x_tl[o][tb]], [self.x_tl[o][tb]],
                                lambda pb=pb, o=o, ts=ts: nc.vector.scalar_tensor_tensor(
                                    self.xT[:, o, ts], po[pb][:], self.gvec[:, o:o + 1], self.xT[:, o, ts], ALU.mult, ALU.add))
                self.barrier()

    def finish(self):
        nc = self.nc
        yv = self.yT.rearrange("(c p) t -> p c t", p=128)
        with contextlib.ExitStack() as es:
            ps_n = es.enter_context(self.pst("ps_nf", [128, 512], F32))
            psn_tl = Tl("psnf")
            sq = [es.enter_context(self.sbt(f"fsq{i}", [128, 512], BF16)) for i in range(2)]
            sq_tl = [Tl("fsq0"), Tl("fsq1")]
            rstd = es.enter_context(self.sbt("frstd", [128, 512], F32))
            rstd_tl = Tl("frstd")
            yb = [es.enter_context(self.sbt(f"yb{i}", [128, 512], F32)) for i in range(4)]
            yb_tl = [Tl(f"yb{i}") for i in range(4)]
            k = 0
            for tb in range(NTB):
                ts = slice(tb * 512, (tb + 1) * 512)
                if self.final_norm:
                    for kc in range(KC):
                        b = kc % 2
                        self.op(self.ACT, [self.x_tl[kc][tb]], [sq_tl[b]],
                                lambda kc=kc, b=b, ts=ts: nc.scalar.activation(sq[b][:], self.xT[:, kc, ts], AF.Square))
                        self.op(self.PE, [sq_tl[b], self.tl_const], [psn_tl],
                                lambda kc=kc, b=b: nc.tensor.matmul(ps_n[:], self.ones[:], sq[b][:], start=(kc == 0), stop=(kc == KC - 1)))
                    self.op(self.ACT, [psn_tl, self.tl_const], [rstd_tl],
                            lambda: nc.scalar.activation(rstd[:], ps_n[:], AF.Sqrt, bias=self.cvec[:, 0:1], scale=1.0 / D))
                    self.op(self.DVE, [rstd_tl], [rstd_tl], lambda: nc.vector.reciprocal(rstd[:], rstd[:]))
                for kc in range(KC):
                    b = k % 4
                    k += 1
                    if self.final_norm:
                        self.op(self.DVE, [self.x_tl[kc][tb], rstd_tl, self.tl_const], [yb_tl[b]],
                                lambda kc=kc, b=b, ts=ts: nc.vector.scalar_tensor_tensor(yb[b][:], self.xT[:, kc, ts], self.fn[:, kc:kc + 1],
                                                                                         rstd[:], ALU.mult, ALU.mult))
                        self.dma(self.SP, self.st_sems[b], [yb_tl[b]], [],
                                 lambda kc=kc, b=b, ts=ts: nc.sync.dma_start(out=yv[:, kc, ts], in_=yb[b][:]))
                    else:
                        self.dma(self.SP, self.st_sems[b], [self.x_tl[kc][tb]], [],
                                 lambda kc=kc, ts=ts: nc.sync.dma_start(out=yv[:, kc, ts], in_=self.xT[:, kc, ts]))
            for d in self.st_sems:
                self.SP.eng.wait_ge(d.sem, d.cnt)
                self.SP.seen[d] = d.cnt
            self.barrier()


def _prep_common(inputs):
    L = DEPTH
    f = lambda a: np.ascontiguousarray(a, dtype=np.float32)

    def vecT(a):
        return f(a.reshape(a.shape[0], KC, 128).transpose(0, 2, 1))

    def winr(w):
        w = w.reshape(L, KC, 128, 2, NFC, 128)
        return f(w.transpose(0, 4, 2, 1, 3, 5).reshape(L, NFC, 128, KC * 256))
    com = {
        "ada_w": f(inputs["ada_w"]),
        "ada_bT": f(inputs["ada_b"].reshape(L, 72, 128).transpose(0, 2, 1)),
        "ng1": vecT(inputs["norm_ffn1"]),
        "ngm": vecT(inputs["norm_mix"]),
        "ng2": vecT(inputs["norm_ffn2"]),
        "w1i": winr(np.asarray(inputs["ffn1_w_in"])),
        "w1o": f(inputs["ffn1_w_out"]),
        "w2i": winr(np.asarray(inputs["ffn2_w_in"])),
        "w2o": f(inputs["ffn2_w_out"]),
        "wmi": f(inputs["mix_w_in"]),
        "wmo": f(inputs["mix_w_out"]),
        "sguwT": f(np.asarray(inputs["sgu_w"]).transpose(0, 1, 3, 2)),
        "sgubT": f(np.asarray(inputs["sgu_b"]).transpose(0, 2, 1)),
        "fnT": f(np.asarray(inputs["final_norm"]).reshape(KC, 128).T),
    }
    return com


_NC_CACHE = {}


def kernel(**inputs):
    inputs = {k: np.asarray(v) for k, v in inputs.items()}
    com = _prep_common(inputs)
    B = inputs["x"].shape[0]
    key = ("full", DEPTH)
    if key not in _NC_CACHE:
        _NC_CACHE[key] = K(DEPTH, True).build()
    nc = _NC_CACHE[key]
    in_maps = []
    for b in range(B):
        m = dict(com)
        m["xT"] = np.ascontiguousarray(inputs["x"][b].T, dtype=np.float32)
        m["cT"] = np.ascontiguousarray(inputs["c"][b].reshape(KC, 128).T, dtype=np.float32)
        m["pos"] = np.ascontiguousarray(inputs["positions"][b].reshape(NCH, 128).T, dtype=np.int32)
        in_maps.append(m)
    res = run_bass_kernel_spmd(nc, in_maps, core_ids=list(range(B)))
    out = np.stack([np.ascontiguousarray(res.results[b]["yT"].T) for b in range(B)], axis=0)
    return out.astype(np.float32)
```

```python
import contextlib
import numpy as np
import concourse.bass as bass
import concourse.mybir as mybir
from concourse.bass_utils import run_bass_kernel_spmd

F32 = mybir.dt.float32
BF16 = mybir.dt.bfloat16
I32 = mybir.dt.int32
AF = mybir.ActivationFunctionType
ALU = mybir.AluOpType
AX = mybir.AxisListType

D = 1024
S = 2048
DEPTH = 4
DFF = 2816
NFC = DFF // 128
KC = D // 128
NTB = S // 512
NCH = S // 128
PROJ = 2884
TOPK = 256
NIT = 16
EPS = 1e-6
ROPE_THETA = 500000.0
NEG = -1.0e30


class Src:
    def __init__(self, nc, name, eng=None):
        self.name = name
        self.eng = eng
        self.sem = nc.alloc_semaphore(name=name)
        self.cnt = 0
        self.seen = {}


class Tl:
    def __init__(self, name=""):
        self.name = name
        self.w = None
        self.rd = {}


class K:
    def __init__(self, n_layers, final_norm, first=True, debug_stop=None):
        self.n_layers = n_layers
        self.final_norm = final_norm
        self.debug_stop = debug_stop
        nc = bass.Bass("TRN2", target_bir_lowering=False)
        self.nc = nc
        self.PE = Src(nc, "s_pe", nc.tensor)
        self.ACT = Src(nc, "s_act", nc.scalar)
        self.DVE = Src(nc, "s_dve", nc.vector)
        self.POOL = Src(nc, "s_pool", nc.gpsimd)
        self.SP = Src(nc, "s_sp", nc.sync)
        self.engs = [self.PE, self.ACT, self.DVE, self.POOL, self.SP]
        self.dsems = []
        self.pe_pending = False
        self.uid = 0

    def sbt(self, name, shape, dt):
        self.uid += 1
        return self.nc.sbuf_tensor(f"{name}_{self.uid}", shape, dt)

    def pst(self, name, shape, dt):
        self.uid += 1
        return self.nc.psum_tensor(f"{name}_{self.uid}", shape, dt)

    def _wait(self, E, deps):
        for src, v in deps.items():
            if src is E and E is self.PE:
                continue
            if E.seen.get(src, 0) < v:
                E.eng.wait_ge(src.sem, v)
                E.seen[src] = v

    def _deps(self, reads, writes):
        deps = {}

        def add(s, v):
            if deps.get(s, 0) < v:
                deps[s] = v
        for t in reads:
            if t.w is not None:
                add(*t.w)
        for t in writes:
            if t.w is not None:
                add(*t.w)
            for s, v in t.rd.items():
                add(s, v)
        return deps

    def op(self, E, reads, writes, emit, inc=True):
        px = [t for t in reads if t.name.startswith("p")]
        if px:
            reads = [t for t in reads if not t.name.startswith("p")]
            writes = list(writes) + px
        self._wait(E, self._deps(reads, writes))
        ins = emit()
        if inc:
            E.cnt += 1
            ins.then_inc(E.sem, 1)
            val = E.cnt
            if E is self.PE:
                self.pe_pending = False
        else:
            assert E is self.PE
            val = E.cnt + 1
            self.pe_pending = True
        for t in reads:
            if t.rd.get(E, 0) < val:
                t.rd[E] = val
        for t in writes:
            t.w = (E, val)
            t.rd = {}
        return ins

    def new_dsem(self, name):
        d = Src(self.nc, name)
        self.dsems.append(d)
        return d

    def dma(self, Q, dsem, reads, writes, emit):
        self._wait(Q, self._deps(reads, writes))
        ins = emit()
        dsem.cnt += 16
        ins.then_inc(dsem.sem, 16)
        for t in reads:
            t.rd[dsem] = dsem.cnt
        for t in writes:
            t.w = (dsem, dsem.cnt)
            t.rd = {}

    def barrier(self):
        assert not self.pe_pending
        srcs = self.engs + self.dsems
        for E in self.engs:
            for s2 in srcs:
                if s2 is E:
                    continue
                if s2.cnt > 0 and E.seen.get(s2, 0) < s2.cnt:
                    E.eng.wait_ge(s2.sem, s2.cnt)
                    E.seen[s2] = s2.cnt

    def mm_group(self, out_ap, out_tl, pairs, reads, inc=True):
        n = len(pairs)
        for i, (l, r) in enumerate(pairs):
            last = (i == n - 1)
            self.op(self.PE, reads, [out_tl],
                    lambda l=l, r=r, i=i, last=last: self.nc.tensor.matmul(out_ap, l, r, start=(i == 0), stop=last),
                    inc=(last and inc))

    def build(self):
        nc = self.nc
        L = self.n_layers
        dr = {}

        def din(name, shape, dt=F32):
            dr[name] = nc.dram_tensor(name, shape, dt, kind="ExternalInput").ap()
            return dr[name]
        din("xT", [D, S])
        din("cT", [128, KC])
        din("pos", [128, NCH], I32)
        din("ada_w", [L, D, 9 * D])
        din("ada_bT", [L, 128, 72])
        din("ng1", [L, 128, KC])
        din("ngm", [L, 128, KC])
        din("ng2", [L, 128, KC])
        din("w1i", [L, NFC, 128, KC * 256])
        din("w1o", [L, DFF, D])
        din("w2i", [L, NFC, 128, KC * 256])
        din("w2o", [L, DFF, D])
        din("wmi", [L, D, PROJ])
        din("wmo", [L, D, D])
        din("sguwT", [L, 4, 128, 128])
        din("sgubT", [L, 128, 4])
        din("fnT", [128, KC])
        self.dr = dr
        self.yT = nc.dram_tensor("yT", [D, S], F32, kind="ExternalOutput").ap()

        with contextlib.ExitStack() as es:
            def sb(name, shape, dt):
                return es.enter_context(self.sbt(name, shape, dt))
            self.xT = sb("xT_sb", [128, KC, S], F32)
            self.hT = sb("hT_sb", [128, KC, S], BF16)
            self.x_tl = [[Tl(f"x{o}_{tb}") for tb in range(NTB)] for o in range(KC)]
            self.h_tl = [Tl(f"h{j}") for j in range(NCH)]
            self.ident = sb("ident", [128, 128], BF16)
            self.ones = sb("ones", [128, 128], BF16)
            self.caus_qs = sb("caus_qs", [128, 128], BF16)
            self.negm = sb("negm", [128, 128], F32)
            self.caus_sq = sb("caus_sq", [128, 128], F32)
            self.p2tab = sb("p2tab", [128, NIT], F32)
            self.cosT = sb("cosT", [128, NCH, 1, 8], F32)
            self.sinT = sb("sinT", [128, NCH, 1, 8], F32)
            self.modT = sb("modT", [128, 72], F32)
            self.cb = sb("cb", [128, KC], BF16)
            self.cvec = sb("cvec", [128, 16], F32)
            self.ng = sb("ng", [128, 3, L, KC], F32)
            self.fn = sb("fn_sb", [128, KC], F32)
            self.avec = sb("avec", [128, KC], F32)
            self.bvec = sb("bvec", [128, KC], F32)
            self.gvec = sb("gvec", [128, KC], F32)
            self.adab = sb("adab", [128, 72], F32)
            self.tl_const = Tl("const")
            self.tl_mod = Tl("mod")
            self.tl_ab = Tl("ab")
            self.tl_g = Tl("g")
            self.tl_adab = Tl("adab")
            self.ld_sem = self.new_dsem("d_ld")
            self.misc_sem = self.new_dsem("d_misc")
            self.st_sems = [self.new_dsem(f"d_st{i}") for i in range(4)]
            self.w_sems = [self.new_dsem(f"d_w{i}") for i in range(4)]

            self.setup()
            for l in range(L):
                self.barrier()
                self.ada(l)
                self.barrier()
                self.ffn(l, 0)
                if self.debug_stop == "ffn1":
                    break
                self.barrier()
                self.mixer(l)
                if self.debug_stop in ("mix", "m1"):
                    break
                self.barrier()
                self.ffn(l, 1)
            self.barrier()
            self.finish()
            self.barrier()
        return nc

    def setup(self):
        nc = self.nc
        dr = self.dr
        L = self.n_layers
        xv = dr["xT"].rearrange("(c p) t -> p c t", p=128)
        for o in range(KC):
            self.dma(self.SP, self.ld_sem, [], [self.x_tl[o][tb] for tb in range(NTB)],
                     lambda o=o: nc.sync.dma_start(out=self.xT[:, o, :], in_=xv[:, o, :]))
        for o in range(KC):
            for tb in range(NTB):
                self.x_tl[o][tb].w = (self.ld_sem, self.ld_sem.cnt)
        with contextlib.ExitStack() as es:
            cf = es.enter_context(self.sbt("cf", [128, KC], F32))
            posi = es.enter_context(self.sbt("posi", [128, NCH], I32))
            posf = es.enter_context(self.sbt("posf", [128, NCH], F32))
            idx = es.enter_context(self.sbt("idx", [128, 128], F32))
            ang = es.enter_context(self.sbt("ang", [128, NCH, 8], F32))
            fr = es.enter_context(self.sbt("fr", [128, NCH, 8], F32))
            fi = es.enter_context(self.sbt("fi", [128, NCH, 8], I32))
            ff = es.enter_context(self.sbt("ff", [128, NCH, 8], F32))
            t_in = Tl("setup_in")
            t_a = Tl("setup_a")
            t_b = Tl("setup_b")
            t_c = Tl("setup_c")
            self.dma(self.SP, self.misc_sem, [], [t_in], lambda: nc.sync.dma_start(out=cf[:], in_=dr["cT"]))
            self.dma(self.SP, self.misc_sem, [], [t_in], lambda: nc.sync.dma_start(out=posi[:], in_=dr["pos"]))
            self.dma(self.SP, self.misc_sem, [], [t_in], lambda: nc.sync.dma_start(out=self.fn[:], in_=dr["fnT"]))
            for i, nm in enumerate(["ng1", "ngm", "ng2"]):
                for l in range(L):
                    self.dma(self.SP, self.misc_sem, [], [t_in],
                             lambda i=i, nm=nm, l=l: nc.sync.dma_start(out=self.ng[:, i, l, :], in_=dr[nm][l]))
            self.op(self.POOL, [], [t_a], lambda: nc.gpsimd.iota(idx[:], [[1, 128]], base=0, channel_multiplier=-1,
                                                                 allow_small_or_imprecise_dtypes=True))
            C = self.tl_const
            self.op(self.DVE, [t_a], [C], lambda: nc.vector.tensor_single_scalar(self.ident[:], idx[:], 0.0, ALU.is_equal))
            self.op(self.DVE, [t_a], [C], lambda: nc.vector.tensor_single_scalar(self.caus_qs[:], idx[:], 0.0, ALU.is_le))
            self.op(self.DVE, [t_a], [C], lambda: nc.vector.tensor_scalar(self.negm[:], idx[:], 0.0, NEG, ALU.is_gt, ALU.mult))
            self.op(self.DVE, [t_a], [C], lambda: nc.vector.tensor_single_scalar(self.caus_sq[:], idx[:], 0.0, ALU.is_ge))
            self.op(self.DVE, [], [C], lambda: nc.vector.memset(self.ones[:], 1.0))
            for it in range(1, NIT + 1):
                self.op(self.DVE, [], [C], lambda it=it: nc.vector.memset(self.p2tab[:, it - 1:it], 2.0 ** (1 - it)))
            self.op(self.DVE, [], [C], lambda: nc.vector.memset(self.cvec[:, 0:1], EPS))
            self.op(self.DVE, [], [C], lambda: nc.vector.memset(self.cvec[:, 1:2], 0.0))
            self.op(self.ACT, [t_in], [C], lambda: nc.scalar.activation(self.cb[:], cf[:], AF.Silu))
            self.op(self.DVE, [t_in], [t_b], lambda: nc.vector.tensor_copy(posf[:], posi[:]))
            for i in range(8):
                inv = float(np.float32(ROPE_THETA) ** np.float32(-(2.0 * i) / 16.0))
                self.op(self.DVE, [t_b], [t_c],
                        lambda i=i, inv=inv: nc.vector.tensor_scalar(ang[:, :, i], posf[:], inv, None, ALU.mult))
            two_pi = 2.0 * np.pi
            for which, dst in ((0, self.sinT), (1, self.cosT)):
                off = 0.0 if which == 0 else 0.25
                self.op(self.DVE, [t_c], [t_b],
                        lambda off=off: nc.vector.tensor_scalar(fr[:], ang[:], 1.0 / two_pi, off, ALU.mult, ALU.add))
                self.op(self.DVE, [t_b], [t_a], lambda: nc.vector.tensor_copy(fi[:], fr[:]))
                self.op(self.DVE, [t_a], [t_a], lambda: nc.vector.tensor_copy(ff[:], fi[:]))
                self.op(self.DVE, [t_a, t_b], [t_b], lambda: nc.vector.tensor_tensor(fr[:], fr[:], ff[:], ALU.subtract))
                self.op(self.DVE, [t_b], [t_a], lambda: nc.vector.tensor_single_scalar(ff[:], fr[:], 0.5, ALU.is_gt))
                self.op(self.DVE, [t_a, t_b], [t_b], lambda: nc.vector.tensor_tensor(fr[:], fr[:], ff[:], ALU.subtract))
                self.op(self.DVE, [t_b], [t_a], lambda: nc.vector.tensor_single_scalar(ff[:], fr[:], -0.5, ALU.is_lt))
                self.op(self.DVE, [t_a, t_b], [t_b], lambda: nc.vector.tensor_tensor(fr[:], fr[:], ff[:], ALU.add))
                self.op(self.ACT, [t_b], [C],
                        lambda dst=dst: nc.scalar.activation(dst[:, :, 0, :], fr[:], AF.Sin, scale=two_pi))
            self.barrier()

    def ada(self, l):
        nc = self.nc
        dr = self.dr
        with contextlib.ExitStack() as es:
            bufs = [es.enter_context(self.sbt(f"adaw{i}", [128, KC, D], BF16)) for i in range(2)]
            btl = [Tl(f"adaw{i}") for i in range(2)]
            ps = es.enter_context(self.pst("ps_mod", [128, 512], F32))
            ps_tl = Tl("ps_mod")
            self.dma(self.SP, self.misc_sem, [], [self.tl_adab],
                     lambda: nc.sync.dma_start(out=self.adab[:], in_=dr["ada_bT"][l]))
            wv = dr["ada_w"][l].rearrange("(kc p) n -> p kc n", p=128)
            for v in range(9):
                b = v % 2
                self.dma(self.POOL, self.w_sems[b], [], [btl[b]],
                         lambda v=v, b=b: nc.gpsimd.dma_start(out=bufs[b][:], in_=wv[:, :, v * D:(v + 1) * D]))
                for jc in range(KC):
                    col = v * KC + jc
                    self.mm_group(ps[:, col:col + 1], ps_tl,
                                  [(bufs[b][:, kc, jc * 128:(jc + 1) * 128], self.cb[:, kc:kc + 1]) for kc in range(KC)],
                                  [btl[b], self.tl_const], inc=(jc == KC - 1))
            self.op(self.DVE, [ps_tl, self.tl_adab], [self.tl_mod],
                    lambda: nc.vector.tensor_tensor(self.modT[:], ps[:, 0:72], self.adab[:], ALU.add))
            self.barrier()

    def make_h(self, es, ps_n, ps_n_tl, g_ap, i_shift, i_scale, rstd=None, rstd_tl=None):
        nc = self.nc
        sq = [es.enter_context(self.sbt(f"sq{i}", [128, 512], BF16)) for i in range(2)]
        sq_tl = [Tl("sq0"), Tl("sq1")]
        if rstd is None:
            rstd = es.enter_context(self.sbt("rstd", [128, 512], F32))
            rstd_tl = Tl("rstd")
        t32 = [es.enter_context(self.sbt(f"t32_{i}", [128, 512], F32)) for i in range(2)]
        t32_tl = [Tl("t32a"), Tl("t32b")]
        M = self.modT
        self.op(self.DVE, [self.tl_mod, self.tl_const], [self.tl_ab],
                lambda: nc.vector.scalar_tensor_tensor(self.avec[:], M[:, i_scale * KC:(i_scale + 1) * KC], 1.0, g_ap,
                                                       ALU.add, ALU.mult))
        self.op(self.DVE, [self.tl_mod], [self.tl_ab],
                lambda: nc.vector.tensor_copy(self.bvec[:], M[:, i_shift * KC:(i_shift + 1) * KC]))
        for tb in range(NTB):
            ts = slice(tb * 512, (tb + 1) * 512)
            for kc in range(KC):
                b = kc % 2
                self.op(self.ACT, [self.x_tl[kc][tb]], [sq_tl[b]],
                        lambda kc=kc, b=b, ts=ts: nc.scalar.activation(sq[b][:], self.xT[:, kc, ts], AF.Square))
                self.op(self.PE, [sq_tl[b], self.tl_const], [ps_n_tl],
                        lambda kc=kc, b=b: nc.tensor.matmul(ps_n[:], self.ones[:], sq[b][:], start=(kc == 0), stop=(kc == KC - 1)),
                        inc=True)
            self.op(self.ACT, [ps_n_tl, self.tl_const], [rstd_tl],
                    lambda: nc.scalar.activation(rstd[:], ps_n[:], AF.Sqrt, bias=self.cvec[:, 0:1], scale=1.0 / D))
            self.op(self.DVE, [rstd_tl], [rstd_tl], lambda: nc.vector.reciprocal(rstd[:], rstd[:]))
            for kc in range(KC):
                b = kc % 2
                self.op(self.DVE, [self.x_tl[kc][tb], rstd_tl, self.tl_ab], [t32_tl[b]],
                        lambda kc=kc, b=b, ts=ts: nc.vector.scalar_tensor_tensor(t32[b][:], self.xT[:, kc, ts], self.avec[:, kc:kc + 1],
                                                                                 rstd[:], ALU.mult, ALU.mult))
                self.op(self.ACT, [t32_tl[b], self.tl_ab], [self.h_tl[4 * tb + i] for i in range(4)],
                        lambda kc=kc, b=b, ts=ts: nc.scalar.activation(self.hT[:, kc, ts], t32[b][:], AF.Identity,
                                                                       bias=self.bvec[:, kc:kc + 1], scale=1.0))

    def ffn(self, l, which):
        nc = self.nc
        dr = self.dr
        w_in = dr["w1i" if which == 0 else "w2i"][l]
        w_out = dr["w1o" if which == 0 else "w2o"][l]
        i_sh, i_sc, i_g = (0, 1, 2) if which == 0 else (6, 7, 8)
        g_ap = self.ng[:, 0 if which == 0 else 2, l, :]
        NP = 11
        with contextlib.ExitStack() as es:
            act = es.enter_context(self.sbt("act", [128, NP, S], BF16))
            act_tl = [[Tl(f"act{c}_{tb}") for tb in range(NTB)] for c in range(NP)]
            wout = es.enter_context(self.sbt("wout", [128, NP, D], BF16))
            wout_tl = Tl("wout")
            NB = 3
            win = [es.enter_context(self.sbt(f"win{i}", [128, KC, 256], BF16)) for i in range(NB)]
            win_tl = [Tl(f"win{i}") for i in range(NB)]
            sg = [es.enter_context(self.sbt(f"sg{i}", [128, 512], BF16)) for i in range(2)]
            sg_tl = [Tl("sg0"), Tl("sg1")]
            ps_g = [es.enter_context(self.pst(f"ps_g{i}", [128, 512], F32)) for i in range(2)]
            ps_u = [es.enter_context(self.pst(f"ps_u{i}", [128, 512], F32)) for i in range(2)]
            ps_o = [es.enter_context(self.pst(f"ps_o{i}", [128, 512], F32)) for i in range(2)]
            ps_n = es.enter_context(self.pst("ps_n", [128, 512], F32))
            psg_tl = [Tl("psg0"), Tl("psg1")]
            psu_tl = [Tl("psu0"), Tl("psu1")]
            pso_tl = [Tl("pso0"), Tl("pso1")]
            psn_tl = Tl("psn")

            def load_win(c_glob):
                b = c_glob % NB
                self.dma(self.POOL, self.w_sems[b], [], [win_tl[b]],
                         lambda: nc.gpsimd.dma_start(out=win[b][:], in_=w_in[c_glob].rearrange("p (kc n) -> p kc n", kc=KC)))

            load_win(0)
            load_win(1)
            self.op(self.DVE, [self.tl_mod], [self.tl_g],
                    lambda: nc.vector.tensor_scalar(self.gvec[:], self.modT[:, i_g * KC:(i_g + 1) * KC], 0.5, None, ALU.mult))
            self.make_h(es, ps_n, psn_tl, g_ap, i_sh, i_sc)
            cnt = 0
            ocnt = 0
            for part in range(2):
                self.dma(self.POOL, self.w_sems[3], [], [wout_tl],
                         lambda part=part: nc.gpsimd.dma_start(
                             out=wout[:], in_=w_out[part * NP * 128:(part + 1) * NP * 128, :].rearrange("(c p) n -> p c n", p=128)))
                for ci in range(NP):
                    c = part * NP + ci
                    if c + 2 < NFC:
                        load_win(c + 2)
                    b = c % NB
                    for tb in range(NTB):
                        ts = slice(tb * 512, (tb + 1) * 512)
                        pb = cnt % 2
                        cnt += 1
                        hts = [self.h_tl[4 * tb + i] for i in range(4)]
                        self.mm_group(ps_g[pb][:], psg_tl[pb],
                                      [(win[b][:, kc, 0:128], self.hT[:, kc, ts]) for kc in range(KC)], [win_tl[b]] + hts)
                        self.mm_group(ps_u[pb][:], psu_tl[pb],
                                      [(win[b][:, kc, 128:256], self.hT[:, kc, ts]) for kc in range(KC)], [win_tl[b]] + hts)
                        self.op(self.ACT, [psg_tl[pb]], [sg_tl[pb]],
                                lambda pb=pb: nc.scalar.activation(sg[pb][:], ps_g[pb][:], AF.Silu))
                        self.op(self.DVE, [sg_tl[pb], psu_tl[pb]], [act_tl[ci][tb]],
                                lambda pb=pb, ci=ci, ts=ts: nc.vector.tensor_tensor(act[:, ci, ts], sg[pb][:], ps_u[pb][:], ALU.mult))
                for tb in range(NTB):
                    ts = slice(tb * 512, (tb + 1) * 512)
                    for o in range(KC):
                        pb = ocnt % 2
                        ocnt += 1
                        self.mm_group(ps_o[pb][:], pso_tl[pb],
                                      [(wout[:, ci, o * 128:(o + 1) * 128], act[:, ci, ts]) for ci in range(NP)],
                                      [wout_tl] + [act_tl[ci][tb] for ci in range(NP)])
                        self.op(self.DVE, [pso_tl[pb], self.tl_g, self.x_tl[o][tb]], [self.x_tl[o][tb]],
                                lambda pb=pb, o=o, ts=ts: nc.vector.scalar_tensor_tensor(
                                    self.xT[:, o, ts], ps_o[pb][:], self.gvec[:, o:o + 1], self.xT[:, o, ts], ALU.mult, ALU.add))
            self.barrier()

    def transposes(self, pT, pT_tl, src_ap_fn, src_tl, n):
        nc = self.nc
        for c in range(n):
            self.op(self.PE, [src_tl, self.tl_const], [pT_tl],
                    lambda c=c: nc.tensor.transpose(pT[:, c * 128:(c + 1) * 128], src_ap_fn(c), self.ident[:]),
                    inc=(c == n - 1))

    def rope(self, ps3, dst3, j, H, tmp, tmp_tl, ps_tl, dst_tl):
        nc = self.nc
        cos = self.cosT[:, j, :, :].broadcast_to([128, H, 8])
        sin = self.sinT[:, j, :, :].broadcast_to([128, H, 8])
        x1 = ps3[:, :, 0:8]
        x2 = ps3[:, :, 8:16]
        t1, t2, t3, t4 = [t[:, 0:H, :] for t in tmp]
        C = self.tl_const
        self.op(self.DVE, [ps_tl, C], [tmp_tl[0]], lambda: nc.vector.tensor_tensor(t1, x1, cos, ALU.mult))
        self.op(self.DVE, [ps_tl, C], [tmp_tl[1]], lambda: nc.vector.tensor_tensor(t2, x2, sin, ALU.mult))
        self.op(self.DVE, [tmp_tl[0], tmp_tl[1]], [dst_tl], lambda: nc.vector.tensor_tensor(dst3[:, :, 0:8], t1, t2, ALU.subtract))
        self.op(self.DVE, [ps_tl, C], [tmp_tl[2]], lambda: nc.vector.tensor_tensor(t3, x2, cos, ALU.mult))
        self.op(self.DVE, [ps_tl, C], [tmp_tl[3]], lambda: nc.vector.tensor_tensor(t4, x1, sin, ALU.mult))
        self.op(self.DVE, [tmp_tl[2], tmp_tl[3]], [dst_tl], lambda: nc.vector.tensor_tensor(dst3[:, :, 8:16], t3, t4, ALU.add))
        self.op(self.ACT, [ps_tl], [dst_tl], lambda: nc.scalar.copy(dst3[:, :, 16:64], ps3[:, :, 16:64]))

    def mixer(self, l):
        nc = self.nc
        dr = self.dr
        wmi = dr["wmi"][l].rearrange("(kc p) n -> p kc n", p=128)
        C = self.tl_const
        with contextlib.ExitStack() as es:
            def sb(name, shape, dt):
                return es.enter_context(self.sbt(name, shape, dt))
            qT = sb("qT", [128, 4, S], BF16)
            kT = sb("kT", [128, 4, S], BF16)
            Vx = sb("Vx", [128, NCH, 8, 65], BF16)
            qiT = sb("qiT", [128, 2, S], BF16)
            kiT2 = sb("kiT2", [128, S], BF16)
            wi_all = sb("wi_all", [128, NCH, 4], F32)
            q_tl = [Tl(f"q{j}") for j in range(NCH)]
            k_tl = [Tl(f"k{j}") for j in range(NCH)]
            v_tl = [Tl(f"v{j}") for j in range(NCH)]
            qi_tl = [Tl(f"qi{j}") for j in range(NCH)]
            ki_tl = [Tl(f"ki{j}") for j in range(NCH)]
            wi_tl = [Tl(f"wi{j}") for j in range(NCH)]
            self.op(self.DVE, [self.tl_mod], [self.tl_g],
                    lambda: nc.vector.tensor_copy(self.gvec[:], self.modT[:, 5 * KC:6 * KC]))
            self.op(self.DVE, [], v_tl, lambda: nc.vector.memset(Vx[:, :, :, 64:65], 1.0))

            with contextlib.ExitStack() as e1:
                def sb1(name, shape, dt):
                    return e1.enter_context(self.sbt(name, shape, dt))
                wblk = [sb1(f"wblk{i}", [128, KC, 512], BF16) for i in range(2)]
                wblk_tl = [Tl("wblk0"), Tl("wblk1")]
                va_tok = sb1("va_tok", [128, NCH, 512], BF16)
                va_tl = [Tl(f"va{j}") for j in range(NCH)]
                tok = [sb1(f"tok{i}", [128, 512], BF16) for i in range(2)]
                tok_tl = [Tl("tok0"), Tl("tok1")]
                u_tok = [sb1(f"utok{i}", [128, 512], BF16) for i in range(2)]
                u_tl = [Tl("u0"), Tl("u1")]
                rt = [sb1(f"rt{i}", [128, 8, 8], F32) for i in range(4)]
                rt_tl = [Tl(f"rt{i}") for i in range(4)]
                rstd_m = sb1("rstd_m", [128, 512], F32)
                sguw32 = rstd_m[:, :].rearrange("p (g t) -> p g t", t=128)
                sguw = sb1("sguw", [128, 4, 128], BF16)
                sgub = sb1("sgub", [128, 4], F32)
                sg_tl = Tl("sguw")
                sg32_tl = Tl("sguw32")
                sgb_tl = Tl("sgub")
                pp = [e1.enter_context(self.pst(f"pp{i}", [128, 512], F32)) for i in range(2)]
                pp_tl = [Tl("pp0"), Tl("pp1")]
                pT = [e1.enter_context(self.pst(f"pT{i}", [128, 1024], BF16)) for i in range(2)]
                pT_tl = [Tl("pT0"), Tl("pT1")]
                p_sgu = e1.enter_context(self.pst("p_sgu", [128, 512], F32))
                psgu_tl = Tl("psgu")
                ps_n = e1.enter_context(self.pst("ps_nm", [128, 512], F32))
                psn_tl = Tl("psnm")

                blocks = [("q", 1024, 512), ("k", 1536, 512), ("v", 2048, 512), ("idx", 2560, 324), ("va", 512, 512), ("u", 0, 512)]

                def load_blk(bi):
                    nm, c0, ncol = blocks[bi]
                    b = bi % 2
                    self.dma(self.POOL, self.w_sems[b], [], [wblk_tl[b]],
                             lambda: nc.gpsimd.dma_start(out=wblk[b][:, :, 0:ncol], in_=wmi[:, :, c0:c0 + ncol]))
                load_blk(0)
                self.dma(self.SP, self.misc_sem, [], [sg32_tl],
                         lambda: nc.sync.dma_start(out=sguw32, in_=dr["sguwT"][l].rearrange("g s t -> s g t")))
                self.dma(self.SP, self.misc_sem, [], [sgb_tl],
                         lambda: nc.sync.dma_start(out=sgub[:], in_=dr["sgubT"][l]))
                sg32_tl.w = (self.misc_sem, self.misc_sem.cnt)
                sgb_tl.w = (self.misc_sem, self.misc_sem.cnt)
                for g in range(4):
                    self.op(self.DVE, [sg32_tl, C], [sg_tl],
                            lambda g=g: nc.vector.tensor_tensor(sguw[:, g, :], sguw32[:, g, :], self.caus_sq[:], ALU.mult))
                self.make_h(e1, ps_n, psn_tl, self.ng[:, 1, l, :], 3, 4, rstd=rstd_m, rstd_tl=sg32_tl)
                cnt = 0
                for bi, (nm, c0, ncol) in enumerate(blocks):
                    if bi + 1 < len(blocks):
                        load_blk(bi + 1)
                    b = bi % 2
                    for j in range(NCH):
                        js = slice(j * 128, (j + 1) * 128)
                        pb = cnt % 2
                        cnt += 1
                        ps = pp[pb]
                        self.mm_group(ps[:, 0:ncol], pp_tl[pb],
                                      [(self.hT[:, kc, js], wblk[b][:, kc, 0:ncol]) for kc in range(KC)],
                                      [self.h_tl[j], wblk_tl[b]])
                        if nm in ("q", "k"):
                            dstT, d_tl = (qT, q_tl) if nm == "q" else (kT, k_tl)
                            ps3 = ps[:, 0:512].rearrange("p (h d) -> p h d", d=64)
                            d3 = tok[pb][:, 0:512].rearrange("p (h d) -> p h d", d=64)
                            self.rope(ps3, d3, j, 8, rt, rt_tl, pp_tl[pb], tok_tl[pb])
                            self.transposes(pT[pb], pT_tl[pb], lambda c, pb=pb: tok[pb][:, c * 128:(c + 1) * 128], tok_tl[pb], 4)
                            self.op(self.ACT if j % 2 else self.DVE, [pT_tl[pb]], [d_tl[j]],
                                    (lambda pb=pb, js=js, dstT=dstT: nc.scalar.copy(dstT[:, :, js], pT[pb][:, 0:512].rearrange("p (c t) -> p c t", t=128)))
                                    if j % 2 else
                                    (lambda pb=pb, js=js, dstT=dstT: nc.vector.tensor_copy(dstT[:, :, js], pT[pb][:, 0:512].rearrange("p (c t) -> p c t", t=128))))
                        elif nm == "v":
                            self.op(self.ACT, [pp_tl[pb]], [v_tl[j]],
                                    lambda pb=pb, j=j: nc.scalar.copy(Vx[:, j, :, 0:64], pp[pb][:, 0:512].rearrange("p (h d) -> p h d", d=64)))
                        elif nm == "idx":
                            ps3 = ps[:, 0:256].rearrange("p (h d) -> p h d", d=64)
                            d3 = tok[pb][:, 0:256].rearrange("p (h d) -> p h d", d=64)
                            self.rope(ps3, d3, j, 4, rt, rt_tl, pp_tl[pb], tok_tl[pb])
                            ps3k = ps[:, 256:320].rearrange("p (h d) -> p h d", d=64)
                            d3k = tok[pb][:, 256:320].rearrange("p (h d) -> p h d", d=64)
                            self.rope(ps3k, d3k, j, 1, rt, rt_tl, pp_tl[pb], tok_tl[pb])
                            self.op(self.DVE, [tok_tl[pb]], [tok_tl[pb]],
                                    lambda pb=pb: nc.vector.tensor_copy(tok[pb][:, 320:384], tok[pb][:, 256:320]))
                            self.op(self.DVE, [pp_tl[pb]], [wi_tl[j]],
                                    lambda pb=pb, j=j: nc.vector.tensor_copy(wi_all[:, j, :], pp[pb][:, 320:324]))
                            self.transposes(pT[pb], pT_tl[pb], lambda c, pb=pb: tok[pb][:, c * 128:(c + 1) * 128], tok_tl[pb], 3)
                            self.op(self.ACT, [pT_tl[pb]], [qi_tl[j]],
                                    lambda pb=pb, js=js: nc.scalar.copy(qiT[:, :, js], pT[pb][:, 0:256].rearrange("p (c t) -> p c t", t=128)))
                            self.op(self.DVE, [pT_tl[pb]], [ki_tl[j]],
                                    lambda pb=pb, js=js: nc.vector.tensor_copy(kiT2[:, js], pT[pb][:, 256:384]))
                        elif nm == "va":
                            self.op(self.ACT, [pp_tl[pb]], [va_tl[j]],
                                    lambda pb=pb, j=j: nc.scalar.activation(va_tok[:, j, :], pp[pb][:, 0:512], AF.Gelu))
                        else:
                            self.op(self.ACT, [pp_tl[pb]], [u_tl[pb]],
                                    lambda pb=pb: nc.scalar.activation(u_tok[pb][:], pp[pb][:, 0:512], AF.Gelu))
                            for g in range(4):
                                gs = slice(g * 128, (g + 1) * 128)
                                self.op(self.PE, [sg_tl, va_tl[j]], [psgu_tl],
                                        lambda g=g, gs=gs, j=j: nc.tensor.matmul(p_sgu[:, gs], sguw[:, g, :], va_tok[:, j, gs], start=True, stop=True),
                                        inc=(g == 3))
                            for g in range(4):
                                gs = slice(g * 128, (g + 1) * 128)
                                self.op(self.DVE, [psgu_tl, u_tl[pb], sgb_tl], [tok_tl[pb]],
                                        lambda g=g, gs=gs, pb=pb: nc.vector.scalar_tensor_tensor(
                                            tok[pb][:, gs], p_sgu[:, gs], sgub[:, g:g + 1], u_tok[pb][:, gs], ALU.add, ALU.mult))
                            self.transposes(pT[pb], pT_tl[pb], lambda c, pb=pb: tok[pb][:, c * 128:(c + 1) * 128], tok_tl[pb], 4)
                            self.op(self.ACT, [pT_tl[pb]], [self.h_tl[j]],
                                    lambda pb=pb, js=js: nc.scalar.copy(self.hT[:, 0:4, js], pT[pb][:, 0:512].rearrange("p (c t) -> p c t", t=128)))
                self.barrier()

            with contextlib.ExitStack() as e2:
              if self.debug_stop != "m1":
                def sb2(name, shape, dt):
                    return e2.enter_context(self.sbt(name, shape, dt))
                score = [sb2(f"score{i}", [128, S], F32) for i in range(3)]
                score_tl = [Tl("score0"), Tl("score1"), Tl("score2")]
                junk = sb2("junk", [128, S], mybir.dt.int8)
                junk_tl = Tl("junk")
                junkA = sb2("junkA", [128, S], mybir.dt.int8)
                junkA_tl = Tl("junkA")
                cN = sb2("cN", [128, NCH], F32)
                for jj in range(NCH):
                    self.op(self.DVE, [], [C], lambda jj=jj: nc.vector.memset(cN[:, jj:jj + 1], float((jj + 1) * 128 - (2 * TOPK - 1))))
                mask = sb2("mask", [128, S], BF16)
                mask_tl = Tl("mask")
                maskT = [sb2(f"maskT{i}", [128, NCH, 128], BF16) for i in range(2)]
                maskT_tl = [Tl("maskT0"), Tl("maskT1")]
                Rb = [sb2(f"Rb{i}", [128, 512], F32) for i in range(2)]
                Rb_tl = [Tl("Rb0"), Tl("Rb1")]
                NPT = 2
                PT = [sb2(f"PT{i}", [128, 512], BF16) for i in range(NPT)]
                PT_tl = [Tl(f"PT{i}") for i in range(NPT)]
                b_tok = [sb2("btok0", [128, 512], BF16)] * 2
                b_tl = [Tl("btok0")] * 2
                sm = sb2("sm", [128, 3, 8], F32)
                sm_tl = [Tl("sm0"), Tl("sm1"), Tl("sm2")]
                wtab = sb2("wtab", [128, 3, NIT], F32)
                rden = sb2("rden", [128, 8], F32)
                rden_tl = Tl("rden")
                pL = [e2.enter_context(self.pst(f"pL{i}", [128, 512], F32)) for i in range(2)]
                pL_tl = [Tl("pL0"), Tl("pL1")]
                pS = [e2.enter_context(self.pst(f"pS{i}", [128, 512], F32)) for i in range(2)]
                pS_tl = [Tl("pS0"), Tl("pS1")]
                pO = [e2.enter_context(self.pst(f"pO{i}", [128, 512], F32)) for i in range(2)]
                pO_tl = [Tl("pO0"), Tl("pO1")]
                pT = [e2.enter_context(self.pst(f"pT2_{i}", [128, 1024], BF16)) for i in range(2)]
                pT_tl = [Tl("pT2_0"), Tl("pT2_1")]
                for p in range(3):
                    self.op(self.DVE, [], [sm_tl[p]], lambda p=p: nc.vector.memset(sm[:, p, 5:6], -1.0e29))
                st = {"scnt": 0, "tcnt": 0}

                def indexer(j):
                    units = []
                    p = j % 3
                    js = slice(j * 128, (j + 1) * 128)
                    n_s = (j + 1) * 128
                    nblk = (n_s + 511) // 512
                    for sbk in range(nblk):
                        w = min(512, n_s - sbk * 512)
                        ss = slice(sbk * 512, sbk * 512 + w)
                        for h in range(4):
                            units.append(lambda sbk=sbk, w=w, ss=ss, h=h: idx_unit(j, p, js, sbk, w, ss, h))
                    return units

                def idx_unit(j, p, js, sbk, w, ss, h):
                        if True:
                            hp, e = h // 2, h % 2
                            ps_ = slice(e * 64, (e + 1) * 64)
                            self.op(self.PE, [qi_tl[j]] + ki_tl[sbk * 4:sbk * 4 + (w // 128)], [pL_tl[e]],
                                    lambda e=e, hp=hp, ps_=ps_, ss=ss, w=w: nc.tensor.matmul(
                                        pL[e][:, 0:w], qiT[ps_, hp, js], kiT2[ps_, ss], start=True, stop=True))
                            self.op(self.ACT, [pL_tl[e]], [Rb_tl[e]],
                                    lambda e=e, w=w: nc.scalar.activation(Rb[e][:, 0:w], pL[e][:, 0:w], AF.Relu))
                            if h == 0:
                                self.op(self.DVE, [Rb_tl[e], wi_tl[j]], [score_tl[p]],
                                        lambda e=e, w=w, ss=ss: nc.vector.tensor_scalar(
                                            score[p][:, ss], Rb[e][:, 0:w], wi_all[:, j, 0:1], None, ALU.mult))
                            else:
                                self.op(self.DVE, [Rb_tl[e], wi_tl[j], score_tl[p]], [score_tl[p]],
                                        lambda e=e, w=w, ss=ss, h=h: nc.vector.scalar_tensor_tensor(
                                            score[p][:, ss], Rb[e][:, 0:w], wi_all[:, j, h:h + 1], score[p][:, ss], ALU.mult, ALU.add))

                def chain(j):
                    p = j % 3
                    js = slice(j * 128, (j + 1) * 128)
                    n_s = (j + 1) * 128
                    sc = score[p]
                    stl = score_tl[p]
                    mt = sm_tl[p]
                    use_act = (j % 2 == 1)
                    units = []

                    def pre():
                        if j >= 2:
                            self.op(self.DVE, [stl], [mt],
                                    lambda: nc.vector.tensor_reduce(sm[:, p, 0:1], sc[:, 0:n_s], AX.X, ALU.max, apply_absolute_value=True))
                        self.op(self.DVE, [stl, C], [stl],
                                lambda: nc.vector.tensor_tensor(sc[:, js], sc[:, js], self.negm[:], ALU.add))
                        if j >= 2:
                            if not use_act:
                                self.op(self.DVE, [mt, C], [mt],
                                        lambda: nc.vector.tensor_scalar(wtab[:, p, :], self.p2tab[:], sm[:, p, 0:1], None, ALU.mult))
                                self.op(self.DVE, [mt], [mt], lambda: nc.vector.memset(sm[:, p, 1:2], 0.0))
                            else:
                                self.op(self.DVE, [mt, C], [mt],
                                        lambda: nc.vector.tensor_scalar(wtab[:, p, :], self.p2tab[:], sm[:, p, 0:1], -0.5, ALU.mult, ALU.mult))
                                self.op(self.DVE, [mt], [mt], lambda: nc.vector.memset(sm[:, p, 1:2], 0.0))

                    def it_dve(it):
                        self.op(self.DVE, [mt, stl], [mt, junk_tl],
                                lambda: nc.vector.tensor_scalar(junk[:, 0:n_s], sc[:, 0:n_s], sm[:, p, 1:2], 0.0,
                                                                ALU.is_ge, ALU.add, accum_out=sm[:, p, 2:3]))
                        self.op(self.DVE, [mt], [mt],
                                lambda: nc.vector.tensor_scalar(sm[:, p, 3:4], sm[:, p, 2:3], TOPK - 0.5, 0.5, ALU.is_ge, ALU.subtract))
                        self.op(self.DVE, [mt], [mt],
                                lambda: nc.vector.scalar_tensor_tensor(sm[:, p, 1:2], sm[:, p, 3:4], wtab[:, p, it - 1:it],
                                                                       sm[:, p, 1:2], ALU.mult, ALU.add))

                    def it_act(it):
                        src_c = 1 if (it % 2 == 1) else 6
                        dst_c = 6 if (it % 2 == 1) else 1
                        self.op(self.ACT, [mt, stl], [mt, junkA_tl],
                                lambda: nc.scalar.activation(junkA[:, 0:n_s], sc[:, 0:n_s], AF.Sign, bias=sm[:, p, src_c:src_c + 1], scale=1.0,
                                                             accum_out=sm[:, p, 2:3]))
                        self.op(self.ACT, [mt], [mt],
                                lambda: nc.scalar.activation(sm[:, p, 3:4], sm[:, p, 2:3], AF.Sign, bias=cN[:, j:j + 1], scale=1.0))
                        self.op(self.ACT, [mt], [mt],
                                lambda: nc.scalar.activation(sm[:, p, dst_c:dst_c + 1], sm[:, p, 3:4], AF.Identity,
                                                             bias=sm[:, p, src_c:src_c + 1], scale=wtab[:, p, it - 1:it]))

                    if j >= 2:
                        for it in range(1, NIT + 1):
                            units.append((lambda it=it: it_act(it)) if use_act else (lambda it=it: it_dve(it)))

                    def post():
                        if j >= 2:
                            if not use_act:
                                self.op(self.DVE, [mt], [mt],
                                        lambda: nc.vector.scalar_tensor_tensor(sm[:, p, 4:5], sm[:, p, 0:1], -(2.0 ** (-NIT)), sm[:, p, 1:2],
                                                                               ALU.mult, ALU.add))
                                self.op(self.DVE, [mt, stl], [mask_tl],
                                        lambda: nc.vector.tensor_scalar(mask[:, 0:n_s], sc[:, 0:n_s], sm[:, p, 4:5], None, ALU.is_ge))
                            else:
                                fin_c = 6 if (NIT % 2 == 1) else 1
                                self.op(self.DVE, [mt, stl], [mask_tl],
                                        lambda: nc.vector.tensor_scalar(mask[:, 0:n_s], sc[:, 0:n_s], sm[:, p, fin_c:fin_c + 1],
                                                                        wtab[:, p, NIT - 1:NIT], ALU.add, ALU.is_ge))
                        else:
                            self.op(self.DVE, [mt, stl], [mask_tl],
                                    lambda: nc.vector.tensor_scalar(mask[:, 0:n_s], sc[:, 0:n_s], sm[:, p, 5:6], None, ALU.is_ge))
                    return pre, units, post

                def mask_transposes(j):
                    p = j % 2
                    for i0 in range(0, j + 1, 4):
                        n = min(4, j + 1 - i0)
                        tb_ = st["tcnt"] % 2
                        st["tcnt"] += 1
                        self.transposes(pT[tb_], pT_tl[tb_], lambda c, i0=i0: mask[:, (i0 + c) * 128:(i0 + c + 1) * 128], mask_tl, n)
                        self.op(self.ACT, [pT_tl[tb_]], [maskT_tl[p]],
                                lambda tb_=tb_, i0=i0, n=n: nc.scalar.copy(maskT[p][:, i0:i0 + n, :],
                                                                           pT[tb_][:, 0:n * 128].rearrange("p (c t) -> p c t", t=128)))

                def attention(j):
                    descs = []
                    for h in range(8):
                        for i0 in range(0, j + 1, 4):
                            descs.append((h, i0, st["scnt"] % 2, st["scnt"] % NPT))
                            st["scnt"] += 1
                    units = []
                    for k, dsc in enumerate(descs):
                        if k == 0:
                            units.append(lambda dsc=dsc: att_A(j, *dsc))
                        else:
                            units.append(lambda dsc=dsc, prev=descs[k - 1]: (att_A(j, *dsc), att_B(j, *prev)))
                    units.append(lambda last=descs[-1]: (att_B(j, *last), att_tail(j)))
                    return units

                def att_A(j, h, i0, sbf, pbf):
                    p = j % 2
                    js = slice(j * 128, (j + 1) * 128)
                    c, e = h // 2, h % 2
                    ps_ = slice(e * 64, (e + 1) * 64)
                    n = min(4, j + 1 - i0)
                    for t in range(n):
                        i = i0 + t
                        self.op(self.PE, [k_tl[i], q_tl[j]], [pS_tl[sbf]],
                                lambda t=t, i=i: nc.tensor.matmul(
                                    pS[sbf][:, t * 128:(t + 1) * 128], kT[ps_, c, i * 128:(i + 1) * 128], qT[ps_, c, js],
                                    start=True, stop=True),
                                inc=(t == n - 1))
                    self.op(self.ACT, [pS_tl[sbf]], [PT_tl[pbf]],
                            lambda: nc.scalar.activation(PT[pbf][:, 0:n * 128], pS[sbf][:, 0:n * 128], AF.Exp, scale=0.125))
                    self.op(self.POOL, [PT_tl[pbf], maskT_tl[p]], [PT_tl[pbf]],
                            lambda: nc.gpsimd.tensor_tensor(
                                PT[pbf][:, 0:n * 128].rearrange("p (c t) -> p c t", t=128),
                                PT[pbf][:, 0:n * 128].rearrange("p (c t) -> p c t", t=128),
                                maskT[p][:, i0:i0 + n, :], ALU.mult))

                def att_B(j, h, i0, sbf, pbf):
                    ob = h // 4
                    oc = (h % 4) * 65
                    n = min(4, j + 1 - i0)
                    for t in range(n):
                        i = i0 + t
                        self.op(self.PE, [PT_tl[pbf], v_tl[i]], [pO_tl[ob]],
                                lambda t=t, i=i: nc.tensor.matmul(
                                    pO[ob][:, oc:oc + 65], PT[pbf][:, t * 128:(t + 1) * 128], Vx[:, i, h, :],
                                    start=(i == 0), stop=(i == j)),
                                inc=(t == n - 1))

                def att_tail(j):
                    js = slice(j * 128, (j + 1) * 128)
                    bb = j % 2
                    for ob in range(2):
                        self.op(self.DVE, [pO_tl[ob]], [rden_tl],
                                lambda ob=ob: nc.vector.reciprocal(rden[:, ob * 4:(ob + 1) * 4],
                                                                   pO[ob][:, 0:260].rearrange("p (h d) -> p h d", d=65)[:, :, 64]))
                    for h in range(8):
                        ob = h // 4
                        oc = (h % 4) * 65
                        self.op(self.ACT, [pO_tl[ob], rden_tl], [b_tl[bb]],
                                lambda h=h, ob=ob, oc=oc: nc.scalar.activation(
                                    b_tok[bb][:, h * 64:(h + 1) * 64], pO[ob][:, oc:oc + 64], AF.Copy, scale=rden[:, h:h + 1]))
                    tb_ = st["tcnt"] % 2
                    st["tcnt"] += 1
                    self.transposes(pT[tb_], pT_tl[tb_], lambda c: b_tok[bb][:, c * 128:(c + 1) * 128], b_tl[bb], 4)
                    self.op(self.ACT, [pT_tl[tb_]], [self.h_tl[j]],
                            lambda tb_=tb_: nc.scalar.copy(self.hT[:, 4:8, js], pT[tb_][:, 0:512].rearrange("p (c t) -> p c t", t=128)))

                def run_merged(lists):
                    items = []
                    for li, us in enumerate(lists):
                        for k, u in enumerate(us):
                            items.append(((k + 0.5) / len(us), li, k, u))
                    items.sort(key=lambda x: (x[0], x[1], x[2]))
                    for _, _, _, u in items:
                        u()

                for jj in (0, 1):
                    for u in indexer(jj):
                        u()
                chains = {}
                chains[0] = chain(0)
                chains[0][0]()
                for u in chains[0][1]:
                    u()
                for j in range(NCH + 1):
                    lists = []
                    if j + 1 < NCH:
                        chains[j + 1] = chain(j + 1)
                        chains[j + 1][0]()
                        lists.append(chains[j + 1][1])
                    if j >= 1:
                        lists.append(attention(j - 1))
                    if j + 2 < NCH:
                        lists.append(indexer(j + 2))
                    run_merged([l for l in lists if l])
                    if j < NCH:
                        chains[j][2]()
                        mask_transposes(j)
                self.barrier()

            with contextlib.ExitStack() as e3:
                wmo = e3.enter_context(self.sbt("wmo_sb", [128, KC, D], BF16))
                wmo_tl = Tl("wmo")
                po = [e3.enter_context(self.pst(f"po{i}", [128, 512], F32)) for i in range(2)]
                po_tl = [Tl("po0"), Tl("po1")]
                self.dma(self.POOL, self.w_sems[0], [], [wmo_tl],
                         lambda: nc.gpsimd.dma_start(out=wmo[:], in_=dr["wmo"][l].rearrange("(kc p) n -> p kc n", p=128)))
                ocnt = 0
                for tb in range(NTB):
                    ts = slice(tb * 512, (tb + 1) * 512)
                    for o in range(KC):
                        pb = ocnt % 2
                        ocnt += 1
                        self.mm_group(po[pb][:], po_tl[pb],
                                      [(wmo[:, kc, o * 128:(o + 1) * 128], self.hT[:, kc, ts]) for kc in range(KC)],
                                      [wmo_tl] + [self.h_tl[4 * tb + i] for i in range(4)])
                        self.op(self.DVE, [po_tl[pb], self.tl_g, self.x_tl[o][tb]], [self.x_tl[o][tb]],
                                lambda pb=pb, o=o, ts=ts: nc.vector.scalar_tensor_tensor(
                                    self.xT[:, o, ts], po[pb][:], self.gvec[:, o:o + 1], self.xT[:, o, ts], ALU.mult, ALU.add))
                self.barrier()

    def finish(self):
        nc = self.nc
        yv = self.yT.rearrange("(c p) t -> p c t", p=128)
        with contextlib.ExitStack() as es:
            ps_n = es.enter_context(self.pst("ps_nf", [128, 512], F32))
            psn_tl = Tl("psnf")
            sq = [es.enter_context(self.sbt(f"fsq{i}", [128, 512], BF16)) for i in range(2)]
            sq_tl = [Tl("fsq0"), Tl("fsq1")]
            rstd = es.enter_context(self.sbt("frstd", [128, 512], F32))
            rstd_tl = Tl("frstd")
            yb = [es.enter_context(self.sbt(f"yb{i}", [128, 512], F32)) for i in range(4)]
            yb_tl = [Tl(f"yb{i}") for i in range(4)]
            k = 0
            for tb in range(NTB):
                ts = slice(tb * 512, (tb + 1) * 512)
                if self.final_norm:
                    for kc in range(KC):
                        b = kc % 2
                        self.op(self.ACT, [self.x_tl[kc][tb]], [sq_tl[b]],
                                lambda kc=kc, b=b, ts=ts: nc.scalar.activation(sq[b][:], self.xT[:, kc, ts], AF.Square))
                        self.op(self.PE, [sq_tl[b], self.tl_const], [psn_tl],
                                lambda kc=kc, b=b: nc.tensor.matmul(ps_n[:], self.ones[:], sq[b][:], start=(kc == 0), stop=(kc == KC - 1)))
                    self.op(self.ACT, [psn_tl, self.tl_const], [rstd_tl],
                            lambda: nc.scalar.activation(rstd[:], ps_n[:], AF.Sqrt, bias=self.cvec[:, 0:1], scale=1.0 / D))
                    self.op(self.DVE, [rstd_tl], [rstd_tl], lambda: nc.vector.reciprocal(rstd[:], rstd[:]))
                for kc in range(KC):
                    b = k % 4
                    k += 1
                    if self.final_norm:
                        self.op(self.DVE, [self.x_tl[kc][tb], rstd_tl, self.tl_const], [yb_tl[b]],
                                lambda kc=kc, b=b, ts=ts: nc.vector.scalar_tensor_tensor(yb[b][:], self.xT[:, kc, ts], self.fn[:, kc:kc + 1],
                                                                                         rstd[:], ALU.mult, ALU.mult))
                        self.dma(self.SP, self.st_sems[b], [yb_tl[b]], [],
                                 lambda kc=kc, b=b, ts=ts: nc.sync.dma_start(out=yv[:, kc, ts], in_=yb[b][:]))
                    else:
                        self.dma(self.SP, self.st_sems[b], [self.x_tl[kc][tb]], [],
                                 lambda kc=kc, ts=ts: nc.sync.dma_start(out=yv[:, kc, ts], in_=self.xT[:, kc, ts]))
            for d in self.st_sems:
                self.SP.eng.wait_ge(d.sem, d.cnt)
                self.SP.seen[d] = d.cnt
            self.barrier()


def _prep_common(inputs):
    L = DEPTH
    f = lambda a: np.ascontiguousarray(a, dtype=np.float32)

    def vecT(a):
        return f(a.reshape(a.shape[0], KC, 128).transpose(0, 2, 1))

    def winr(w):
        w = w.reshape(L, KC, 128, 2, NFC, 128)
        return f(w.transpose(0, 4, 2, 1, 3, 5).reshape(L, NFC, 128, KC * 256))
    com = {
        "ada_w": f(inputs["ada_w"]),
        "ada_bT": f(inputs["ada_b"].reshape(L, 72, 128).transpose(0, 2, 1)),
        "ng1": vecT(inputs["norm_ffn1"]),
        "ngm": vecT(inputs["norm_mix"]),
        "ng2": vecT(inputs["norm_ffn2"]),
        "w1i": winr(np.asarray(inputs["ffn1_w_in"])),
        "w1o": f(inputs["ffn1_w_out"]),
        "w2i": winr(np.asarray(inputs["ffn2_w_in"])),
        "w2o": f(inputs["ffn2_w_out"]),
        "wmi": f(inputs["mix_w_in"]),
        "wmo": f(inputs["mix_w_out"]),
        "sguwT": f(np.asarray(inputs["sgu_w"]).transpose(0, 1, 3, 2)),
        "sgubT": f(np.asarray(inputs["sgu_b"]).transpose(0, 2, 1)),
        "fnT": f(np.asarray(inputs["final_norm"]).reshape(KC, 128).T),
    }
    return com


_NC_CACHE = {}


def kernel(**inputs):
    inputs = {k: np.asarray(v) for k, v in inputs.items()}
    com = _prep_common(inputs)
    B = inputs["x"].shape[0]
    key = ("full", DEPTH)
    if key not in _NC_CACHE:
        _NC_CACHE[key] = K(DEPTH, True).build()
    nc = _NC_CACHE[key]
    in_maps = []
    for b in range(B):
        m = dict(com)
        m["xT"] = np.ascontiguousarray(inputs["x"][b].T, dtype=np.float32)
        m["cT"] = np.ascontiguousarray(inputs["c"][b].reshape(KC, 128).T, dtype=np.float32)
        m["pos"] = np.ascontiguousarray(inputs["positions"][b].reshape(NCH, 128).T, dtype=np.int32)
        in_maps.append(m)
    res = run_bass_kernel_spmd(nc, in_maps, core_ids=list(range(B)))
    out = np.stack([np.ascontiguousarray(res.results[b]["yT"].T) for b in range(B)], axis=0)
    return out.astype(np.float32)
```
